# Optimizing a Trainium2 kernel written in Bass

```python
import math
import jax, jax.numpy as jnp
from jax import lax
import numpy as np

D_MODEL = 1024
BATCH = 16
SEQ = 4096
DEPTH = 1

HEAD_DIM = 64
HALF_DIM = HEAD_DIM // 2
A_HEADS = 8
A_WIDTH = A_HEADS * HEAD_DIM
DILATED_PATTERNS = ((128, 1), (512, 4), (2048, 16))
B_HEADS = 4
B_QK_WIDTH = B_HEADS * 2 * HEAD_DIM
B_V_WIDTH = B_HEADS * 2 * HEAD_DIM
IN_WIDTHS = (A_WIDTH, A_WIDTH, A_WIDTH, B_QK_WIDTH, B_QK_WIDTH, B_V_WIDTH, D_MODEL, D_MODEL)
IN_WIDTH = 3 * A_WIDTH + 2 * B_QK_WIDTH + B_V_WIDTH + 2 * D_MODEL
D_FF = 4 * D_MODEL
ROPE_THETA = 10000.0
NORM_EPS = 1e-6
SUBLN_EPS = 1e-5
Q_BLOCK = 128

kernel_name = "hybrid_dilated_diffattn_gated_block"


def lambda_init_fn(layer):
    return 0.8 - 0.6 * math.exp(-0.3 * layer)


def rms_norm(x, g, eps):
    xf = x.astype(jnp.float32)
    y = xf * lax.rsqrt(jnp.mean(xf * xf, axis=-1, keepdims=True) + eps)
    return (y * g.astype(jnp.float32)).astype(x.dtype)


def rope_tables(seq, dtype):
    pos = jnp.arange(seq, dtype=jnp.float32)
    inv_freq = ROPE_THETA ** (-jnp.arange(0, HEAD_DIM, 2, dtype=jnp.float32) / HEAD_DIM)
    ang = pos[:, None] * inv_freq[None, :]
    ang = jnp.concatenate([ang, ang], axis=-1)
    return jnp.cos(ang).astype(dtype), jnp.sin(ang).astype(dtype)


def apply_rope(t, cos, sin):
    shp = t.shape
    t4 = t.reshape(shp[0], shp[1], -1, HEAD_DIM)
    rot = jnp.concatenate([-t4[..., HALF_DIM:], t4[..., :HALF_DIM]], axis=-1)
    out = t4 * cos[None, :, None, :] + rot * sin[None, :, None, :]
    return out.reshape(shp)


def banded_window_attention(q, k, v, half):
    n, L, h, dh = q.shape
    blk = half
    nb = -(-L // blk)
    lp = nb * blk
    qb = jnp.pad(q, ((0, 0), (0, lp - L), (0, 0), (0, 0))).reshape(n, nb, blk, h, dh)
    pad_kv = ((0, 0), (blk, lp - L + blk), (0, 0), (0, 0))
    kp = jnp.pad(k, pad_kv).reshape(n, nb + 2, blk, h, dh)
    vp = jnp.pad(v, pad_kv).reshape(n, nb + 2, blk, h, dh)
    kw = jnp.concatenate([kp[:, :-2], kp[:, 1:-1], kp[:, 2:]], axis=2)
    vw = jnp.concatenate([vp[:, :-2], vp[:, 1:-1], vp[:, 2:]], axis=2)
    qpos = jnp.arange(nb)[:, None] * blk + jnp.arange(blk)[None, :]
    kpos = (jnp.arange(nb)[:, None] - 1) * blk + jnp.arange(3 * blk)[None, :]
    rel = kpos[:, None, :] - qpos[:, :, None]
    valid = (jnp.abs(rel) <= half) & (kpos[:, None, :] >= 0) & (kpos[:, None, :] < L)
    scale = 1.0 / math.sqrt(dh)
    s = jnp.einsum('ncqhd,nckhd->nhcqk', qb, kw).astype(jnp.float32) * scale
    s = jnp.where(valid[None, None], s, -jnp.inf)
    lse = jax.nn.logsumexp(s, axis=-1)
    p = jnp.exp(s - lse[..., None])
    o = jnp.einsum('nhcqk,nckhd->ncqhd', p.astype(v.dtype), vw).reshape(n, lp, h, dh)[:, :L]
    lse = lse.transpose(0, 2, 3, 1).reshape(n, lp, h)[:, :L]
    return o, lse


def dilated_window_attention(q, k, v):
    b, s, h, dh = q.shape
    outs, lses = [], []
    for window, dil in DILATED_PATTERNS:
        half = window // (2 * dil)
        ls = s // dil

        def to_phase(t):
            return t.reshape(b, ls, dil, h, dh).transpose(0, 2, 1, 3, 4).reshape(b * dil, ls, h, dh)

        o, lse = banded_window_attention(to_phase(q), to_phase(k), to_phase(v), half)
        outs.append(o.reshape(b, dil, ls, h, dh).transpose(0, 2, 1, 3, 4).reshape(b, s, h, dh))
        lses.append(lse.reshape(b, dil, ls, h).transpose(0, 2, 1, 3).reshape(b, s, h))
    w = jax.nn.softmax(jnp.stack(lses, axis=0), axis=0)
    return jnp.einsum('pbsh,pbshd->bshd', w.astype(q.dtype), jnp.stack(outs, axis=0))


def differential_attention(q, k, v, lam, lam_init, subln_g):
    b, s, h, _, dh = q.shape
    nq = s // Q_BLOCK
    scale = 1.0 / math.sqrt(dh)
    qblocks = q.reshape(b, nq, Q_BLOCK, h, 2, dh).transpose(1, 0, 2, 3, 4, 5)

    def block(qb):
        sc = jnp.einsum('bqhcd,bkhcd->bhcqk', qb, k).astype(jnp.float32) * scale
        p = jax.nn.softmax(sc, axis=-1)
        pd = p[:, :, 0] - lam * p[:, :, 1]
        return jnp.einsum('bhqk,bkhe->bqhe', pd.astype(v.dtype), v)

    o = lax.map(block, qblocks)
    o = o.transpose(1, 0, 2, 3, 4).reshape(b, s, h, 2 * dh)
    return rms_norm(o, subln_g, SUBLN_EPS) * (1.0 - lam_init)


def setup_inputs(seed: int = 0) -> dict:
    key = jax.random.key(seed)
    ks = jax.random.split(key, 16)
    f32 = jnp.float32

    def nrm(k, shape, scale):
        return jax.random.normal(k, shape, f32) * scale

    return {
        "x": nrm(ks[0], (BATCH, SEQ, D_MODEL), 1.0),
        "w_in": nrm(ks[1], (DEPTH, D_MODEL, IN_WIDTH), D_MODEL ** -0.5),
        "w_branch_a": nrm(ks[2], (DEPTH, A_WIDTH, D_MODEL), A_WIDTH ** -0.5),
        "w_branch_b": nrm(ks[3], (DEPTH, B_V_WIDTH, D_MODEL), B_V_WIDTH ** -0.5),
        "w_out": nrm(ks[4], (DEPTH, D_MODEL, D_MODEL), D_MODEL ** -0.5),
        "lambda_q1": nrm(ks[5], (DEPTH, HEAD_DIM), 0.1),
        "lambda_k1": nrm(ks[6], (DEPTH, HEAD_DIM), 0.1),
        "lambda_q2": nrm(ks[7], (DEPTH, HEAD_DIM), 0.1),
        "lambda_k2": nrm(ks[8], (DEPTH, HEAD_DIM), 0.1),
        "diff_subln_g": 1.0 + nrm(ks[9], (DEPTH, 2 * HEAD_DIM), 0.02),
        "norm_mix_g": 1.0 + nrm(ks[10], (DEPTH, D_MODEL), 0.02),
        "norm_mlp_g": 1.0 + nrm(ks[11], (DEPTH, D_MODEL), 0.02),
        "w_ff1": nrm(ks[12], (DEPTH, D_MODEL, D_FF), D_MODEL ** -0.5),
        "w_ff2": nrm(ks[13], (DEPTH, D_FF, D_MODEL), D_FF ** -0.5),
        "norm_final_g": 1.0 + nrm(ks[14], (D_MODEL,), 0.02),
    }


def reference(x, w_in, w_branch_a, w_branch_b, w_out, lambda_q1, lambda_k1, lambda_q2, lambda_k2,
              diff_subln_g, norm_mix_g, norm_mlp_g, w_ff1, w_ff2, norm_final_g):
    b, s, _ = x.shape
    cos, sin = rope_tables(s, x.dtype)
    split_idx = np.cumsum(IN_WIDTHS)[:-1].tolist()
    for l in range(DEPTH):
        h_in = rms_norm(x, norm_mix_g[l], NORM_EPS)
        proj = h_in @ w_in[l]
        qa, ka, va, qb, kb, vb, ga, gb = jnp.split(proj, split_idx, axis=-1)
        qa = apply_rope(qa.reshape(b, s, A_HEADS, HEAD_DIM), cos, sin)
        ka = apply_rope(ka.reshape(b, s, A_HEADS, HEAD_DIM), cos, sin)
        va = va.reshape(b, s, A_HEADS, HEAD_DIM)
        qb = apply_rope(qb.reshape(b, s, B_HEADS, 2, HEAD_DIM), cos, sin)
        kb = apply_rope(kb.reshape(b, s, B_HEADS, 2, HEAD_DIM), cos, sin)
        vb = vb.reshape(b, s, B_HEADS, 2 * HEAD_DIM)

        ya = dilated_window_attention(qa, ka, va).reshape(b, s, A_WIDTH) @ w_branch_a[l]

        lam_init = lambda_init_fn(l)
        lam = (jnp.exp(jnp.sum(lambda_q1[l].astype(jnp.float32) * lambda_k1[l].astype(jnp.float32)))
               - jnp.exp(jnp.sum(lambda_q2[l].astype(jnp.float32) * lambda_k2[l].astype(jnp.float32)))
               + lam_init)
        yb = differential_attention(qb, kb, vb, lam, lam_init, diff_subln_g[l]).reshape(b, s, B_V_WIDTH) @ w_branch_b[l]

        merged = jax.nn.sigmoid(ga) * ya + jax.nn.sigmoid(gb) * yb
        x = x + merged @ w_out[l]

        h2 = rms_norm(x, norm_mlp_g[l], NORM_EPS)
        x = x + jnp.square(jax.nn.relu(h2 @ w_ff1[l])) @ w_ff2[l]
    return rms_norm(x, norm_final_g, NORM_EPS)
```

```python
import os
import numpy as np
import ml_dtypes
import concourse.bass as bass
import concourse.mybir as mybir
from concourse.bass_utils import run_bass_kernel_spmd

F32 = mybir.dt.float32
BF16 = mybir.dt.bfloat16
AF = mybir.ActivationFunctionType
ALU = mybir.AluOpType

NCORES = 8
S = 4096
D = 1024
NB = 2
DFF = 4096
NCH = S // 512


class Res:
    __slots__ = ("name", "writer", "readers", "excl")

    def __init__(self, name, excl=False):
        self.name = name
        self.excl = excl
        self.writer = None
        self.readers = {}


class Op:
    __slots__ = ("eng", "fn", "deps", "dma", "semkey", "signal", "event", "idx", "raw")


class Prog:
    ENGS = ("pe", "act", "dve", "pool", "sp")
    SEM_LIMIT = 30000

    def __init__(self, nc):
        self.nc = nc
        self.ops = []
        self.last = {}
        self.last_dma = {}
        self.fence_deps = {}

    def res(self, name=None):
        return Res(name)

    def add(self, eng, fn, reads=(), writes=(), dma=False, semkey=None):
        if fn is not None and len(self.ops) >= int(os.environ.get("MK_MAXOPS", "1000000000")):
            return None
        op = Op()
        op.eng = eng
        op.fn = fn
        op.dma = dma
        op.semkey = semkey
        op.signal = dma
        op.event = None
        op.idx = len(self.ops)
        deps = {}
        raw = set()
        writes = list(writes) + [r for r in reads if r.excl and r not in writes]
        for r in reads:
            if r.writer is not None:
                deps[id(r.writer)] = r.writer
                raw.add(id(r.writer))
        for w in writes:
            if w.writer is not None:
                deps[id(w.writer)] = w.writer
            for d in w.readers.values():
                deps[id(d)] = d
        fd = self.fence_deps.pop(eng, None)
        if fd:
            for d in fd:
                deps[id(d)] = d
        key = ("dma", semkey) if dma else eng
        for r in reads:
            r.readers[key] = op
        for w in writes:
            w.writer = op
            w.readers = {}
        op.deps = list(deps.values())
        op.raw = raw
        self.ops.append(op)
        if dma:
            self.last_dma[semkey] = op
        else:
            self.last[eng] = op
        return op

    def fence(self):
        alld = [o for o in self.last.values() if o.fn is not None] + list(self.last_dma.values())
        for e in self.ENGS:
            self.fence_deps[e] = list(alld)

    def emit(self):
        nc = self.nc
        ops = self.ops
        print("MK ops", len(ops), flush=True)
        for op in ops:
            for d in op.deps:
                if d.dma:
                    continue
                if d.eng == op.eng and not op.dma and (d.eng == "pe" or id(d) not in op.raw):
                    continue
                d.signal = True
        eng_sem, eng_cnt, dma_sem, dma_cnt = {}, {}, {}, {}
        for op in ops:
            if op.fn is None or not op.signal:
                continue
            if op.dma:
                k = op.semkey
                if k not in dma_sem or dma_cnt[k] >= self.SEM_LIMIT:
                    dma_sem[k] = nc.alloc_semaphore(name=f"d_{k}_{op.idx}")
                    dma_cnt[k] = 0
                dma_cnt[k] += 16
                op.event = (dma_sem[k], dma_cnt[k])
            else:
                e = op.eng
                if e not in eng_sem or eng_cnt[e] >= self.SEM_LIMIT:
                    eng_sem[e] = nc.alloc_semaphore(name=f"e_{e}_{op.idx}")
                    eng_cnt[e] = 0
                eng_cnt[e] += 1
                op.event = (eng_sem[e], eng_cnt[e])
        per_eng = {e: [] for e in self.ENGS}
        for op in ops:
            per_eng[op.eng].append(op)

        def run(engobj, lst):
            waited = {}
            for op in lst:
                for d in op.deps:
                    if d.event is None:
                        continue
                    if (not d.dma) and d.eng == op.eng and not op.dma and (d.eng == "pe" or id(d) not in op.raw):
                        continue
                    sem, val = d.event
                    k = id(sem)
                    if waited.get(k, 0) < val:
                        engobj.wait_ge(sem, val)
                        waited[k] = val
                if op.fn is None:
                    continue
                ins = op.fn(engobj)
                if op.signal:
                    ins.then_inc(op.event[0], 16 if op.dma else 1)

        with nc.Block() as block:
            @block.tensor
            def _(e):
                run(e, per_eng["pe"])

            @block.scalar
            def _(e):
                run(e, per_eng["act"])

            @block.vector
            def _(e):
                run(e, per_eng["dve"])

            @block.gpsimd
            def _(e):
                run(e, per_eng["pool"])

            @block.sync
            def _(e):
                run(e, per_eng["sp"])


class Alloc:
    def __init__(self, nc, prefix, base=16512):
        self.nc, self.prefix, self.off = nc, prefix, base

    def t(self, name, shape, dtype):
        n = 1
        for s in shape[1:]:
            n *= s
        nbytes = n * (2 if dtype == BF16 else 4)
        nbytes = (nbytes + 63) // 64 * 64
        h = self.nc.alloc_sbuf_tensor_at(f"{self.prefix}_{name}", list(shape), dtype, offset=self.off)
        self.off += nbytes
        assert self.off <= 229344, (self.prefix, name, self.off)
        return h


class Rot:
    def __init__(self, items):
        self.items = items
        self.i = 0

    def next(self):
        it = self.items[self.i % len(self.items)]
        self.i += 1
        return it


def build(debug=False, stop_after=None):
    nc = bass.Bass("TRN2", target_bir_lowering=False)

    def din(name, shape, dt=F32):
        return nc.dram_tensor(name, list(shape), dt, kind="ExternalInput").ap()

    def dscr(name, shape, dt):
        return nc.dram_tensor(name, list(shape), dt, kind="ExternalOutput" if debug else "Internal").ap()

    x = din("x", [NB * S, D])
    w_in = din("w_in", [D, 5120])
    w_a = din("w_a", [512, D])
    w_b = din("w_b", [512, D])
    w_out = din("w_out", [D, D])
    w_ff1 = din("w_ff1", [D, DFF])
    w_ff2 = din("w_ff2", [DFF, D])
    lamv = din("lamv", [128, 4, 64])
    subg = din("subg", [128, 128])
    gmix = din("gmix", [128, 8])
    gmlp = din("gmlp", [128, 8])
    gfin = din("gfin", [128, D])
    cosT = din("cosT", [128, S])
    sinT = din("sinT", [128, S])
    ident = din("ident", [128, 128], BF16)
    rswap = din("rswap", [128, 128], BF16)
    mask2 = din("mask2", [128, 2, 256], BF16)
    out = nc.dram_tensor("out", [NB * S, D], F32, kind="ExternalOutput").ap()

    QKT = dscr("QKT", [NB, 2048, S], BF16)
    VA = dscr("VA", [NB, 4, S, 130], BF16)
    VB = dscr("VB", [NB, 4, 128, 32, 129], BF16)
    GT = dscr("GT", [NB, 2048, S], BF16)
    OT = dscr("OT", [NB, 1024, S], BF16)
    X1 = dscr("X1", [NB * S, D], F32)
    H2T = dscr("H2T", [NB, 1024, S], BF16)

    P = Prog(nc)
    ps = [nc.alloc_psum_tensor(f"ps{i}", [128, 512], F32) for i in range(8)]
    psr = [Res(f"ps{i}", excl=True) for i in range(8)]
    out_res = []

    def mm(o, lhsT, rhs, start, stop, reads, writes):
        P.add("pe", lambda e: e.matmul(o, lhsT=lhsT, rhs=rhs, start=start, stop=stop), reads=reads, writes=writes)

    A = Alloc(nc, "s1")
    w_in_bf = A.t("w_in_bf", [128, 8, 5120], BF16)
    idb = A.t("idb", [128, 128], BF16)
    rsw = A.t("rsw", [128, 128], BF16)
    cs_sb = [A.t(f"cs{i}", [128, 2, 512], F32) for i in range(2)]
    gmix_sb = A.t("gmix", [128, 8], F32)
    xin = [A.t(f"xin{i}", [128, 4, D], F32) for i in range(2)]
    junk = A.t("junk", [128, D], BF16)
    ss = A.t("ss", [128, 4], F32)
    rstd = A.t("rstd", [128, 4], F32)
    hb = A.t("hb", [128, 4, D], BF16)
    hT = [A.t(f"hT{i}", [128, 8, 512], BF16) for i in range(2)]
    tb = [A.t(f"tb{i}", [128, 512], BF16) for i in range(2)]
    t1 = [A.t(f"t1{i}", [128, 512], F32) for i in range(2)]
    t2 = [A.t(f"t2{i}", [128, 512], F32) for i in range(2)]
    qk_st = [A.t(f"qkst{i}", [128, 4, 512], BF16) for i in range(2)]
    g_st = [A.t(f"gst{i}", [128, 4, 512], BF16) for i in range(2)]
    va_st = [A.t(f"vast{i}", [128, 4, 4, 130], BF16) for i in range(2)]
    vb_st = [A.t(f"vbst{i}", [128, 4, 4, 129], BF16) for i in range(2)]

    r_w = P.res("w_in_bf")
    r_const = P.res("const1")
    r_cs = [P.res() for _ in range(2)]
    r_xin = [P.res() for _ in range(2)]
    r_junk, r_ss, r_rstd, r_hb = P.res(), P.res(), P.res(), P.res()
    r_hT = [P.res() for _ in range(2)]
    tbR = Rot([(tb[i], P.res()) for i in range(2)])
    t1R = Rot([(t1[i], P.res()) for i in range(2)])
    t2R = Rot([(t2[i], P.res()) for i in range(2)])
    qkR = Rot([(qk_st[i], P.res()) for i in range(2)])
    gR = Rot([(g_st[i], P.res()) for i in range(2)])
    vaR = Rot([(va_st[i], P.res()) for i in range(2)])
    vbR = Rot([(vb_st[i], P.res()) for i in range(2)])
    pTR = Rot([(ps[0], psr[0]), (ps[1], psr[1])])
    prR = Rot([(ps[2], psr[2]), (ps[3], psr[3])])
    pjR = Rot([(ps[4], psr[4]), (ps[5], psr[5]), (ps[6], psr[6]), (ps[7], psr[7])])

    for dst, src, k in ((idb, ident, "c_id"), (rsw, rswap, "c_rs"), (gmix_sb, gmix, "c_gm")):
        P.add("sp", lambda e, dst=dst, src=src: e.dma_start(out=dst[:], in_=src), writes=[r_const], dma=True, semkey=k)
    for i in range(2):
        P.add("pool", lambda e, i=i: e.memset(va_st[i][:], 1.0), writes=[vaR.items[i][1]])
        P.add("pool", lambda e, i=i: e.memset(vb_st[i][:], 1.0), writes=[vbR.items[i][1]])
    r_wk = [P.res() for _ in range(8)]
    for kc in range(8):
        P.add("pool", lambda e, kc=kc: e.dma_start(out=w_in_bf[:, kc, :], in_=w_in[kc * 128:(kc + 1) * 128, :]),
              writes=[r_wk[kc]], dma=True, semkey=f"w_in{kc}")
        if kc % 2:
            P.add("dve", lambda e, kc=kc: e.tensor_scalar_mul(out=w_in_bf[:, kc, :], in0=w_in_bf[:, kc, :],
                                                               scalar1=gmix_sb[:, kc:kc + 1]),
                  reads=[r_wk[kc], r_const], writes=[r_wk[kc], r_w])
        else:
            P.add("act", lambda e, kc=kc: e.activation(out=w_in_bf[:, kc, :], in_=w_in_bf[:, kc, :], func=AF.Copy,
                                                        scale=gmix_sb[:, kc:kc + 1]),
                  reads=[r_wk[kc], r_const], writes=[r_wk[kc], r_w])

    QK_COL0 = [0, 128, 256, 384, 512, 640, 768, 896, 1536, 1664, 1792, 1920, 2048, 2176, 2304, 2432]

    def s1_load(ci):
        sl = ci % 2
        P.add("sp", lambda e: e.dma_start(out=xin[sl][:], in_=x[ci * 512:(ci + 1) * 512, :].rearrange(
            "(t p) f -> p t f", p=128)), writes=[r_xin[sl]], dma=True, semkey=f"xin{sl}")
        cc = ci % NCH
        P.add("sp", lambda e: e.dma_start(out=cs_sb[sl][:, 0, :], in_=cosT[:, cc * 512:(cc + 1) * 512]),
              writes=[r_cs[sl]], dma=True, semkey=f"cs{sl}")
        P.add("sp", lambda e: e.dma_start(out=cs_sb[sl][:, 1, :], in_=sinT[:, cc * 512:(cc + 1) * 512]),
              writes=[r_cs[sl]], dma=True, semkey=f"cs{sl}")

    def s1_norm(ci):
        sl = ci % 2
        for tt in range(4):
            P.add("act", lambda e, tt=tt: e.activation(out=junk[:], in_=xin[sl][:, tt, :], func=AF.Square,
                                                       accum_out=ss[:, tt:tt + 1]),
                  reads=[r_xin[sl]], writes=[r_junk, r_ss])
        P.add("dve", lambda e: e.tensor_scalar(out=rstd[:], in0=ss[:], scalar1=1.0 / D, scalar2=1e-6,
                                               op0=ALU.mult, op1=ALU.add), reads=[r_ss], writes=[r_rstd])
        P.add("act", lambda e: e.activation(out=rstd[:], in_=rstd[:], func=AF.Ln), reads=[r_rstd], writes=[r_rstd])
        P.add("act", lambda e: e.activation(out=rstd[:], in_=rstd[:], func=AF.Exp, scale=-0.5),
              reads=[r_rstd], writes=[r_rstd])
        for tt in range(4):
            P.add("act", lambda e, tt=tt: e.activation(out=hb[:, tt, :], in_=xin[sl][:, tt, :], func=AF.Copy,
                                                       scale=rstd[:, tt:tt + 1]),
                  reads=[r_xin[sl], r_rstd], writes=[r_hb])
        for fc in range(8):
            pt, ptr = pTR.next()
            ptb = pt.bitcast(BF16)
            for tt in range(4):
                P.add("pe", lambda e, tt=tt, fc=fc, ptb=ptb: e.transpose(
                    out=ptb[:, tt * 128:(tt + 1) * 128], in_=hb[:, tt, fc * 128:(fc + 1) * 128], identity=idb[:]),
                    reads=[r_hb, r_const], writes=[ptr])
            P.add("dve", lambda e, fc=fc, ptb=ptb: e.tensor_copy(out=hT[sl][:, fc, :], in_=ptb[:, 0:512]),
                  reads=[ptr], writes=[r_hT[sl]])

    def s1_qk(ci):
        sl = ci % 2
        b, cc = divmod(ci, NCH)
        pend = []

        def finish(pq, pqr, tbb, tbr, st, str_, m, grp, last):
            pr, prr = prR.next()
            mm(pr[:], rsw[:], tbb[:], True, True, [tbr, r_const], [prr])
            a1, a1r = t1R.next()
            a2, a2r = t2R.next()
            P.add("dve", lambda e: e.tensor_tensor(out=a1[:], in0=pq[:], in1=cs_sb[sl][:, 0, :], op=ALU.mult),
                  reads=[pqr, r_cs[sl]], writes=[a1r])
            P.add("dve", lambda e: e.tensor_tensor(out=a2[:], in0=pr[:], in1=cs_sb[sl][:, 1, :], op=ALU.mult),
                  reads=[prr, r_cs[sl]], writes=[a2r])
            P.add("pool", lambda e: e.tensor_tensor(out=st[:, m, :], in0=a1[:], in1=a2[:], op=ALU.add),
                  reads=[a1r, a2r], writes=[str_])
            if last:
                P.add("sp", lambda e: e.dma_start(
                    out=QKT[b, grp * 512:(grp + 1) * 512, cc * 512:(cc + 1) * 512].rearrange("(m p) s -> p m s", p=128),
                    in_=st[:]), reads=[str_], writes=[], dma=True, semkey=f"qkst{grp % 2}")

        for grp in range(4):
            st, str_ = qkR.next()
            for m in range(4):
                mc = grp * 4 + m
                c0 = QK_COL0[mc]
                pq, pqr = pjR.next()
                for kc in range(8):
                    mm(pq[:], w_in_bf[:, kc, c0:c0 + 128], hT[sl][:, kc, :], kc == 0, kc == 7,
                       [r_w, r_hT[sl]], [pqr])
                tbb, tbr = tbR.next()
                P.add("act", lambda e, tbb=tbb, pq=pq: e.activation(out=tbb[:], in_=pq[:], func=AF.Copy),
                      reads=[pqr], writes=[tbr])
                if pend:
                    finish(*pend.pop())
                pend.append((pq, pqr, tbb, tbr, st, str_, m, grp, m == 3))
        return [(lambda t=t: finish(*t)) for t in pend]

    def s1_v(ci, pend_qk=None):
        sl = ci % 2
        b, cc = divmod(ci, NCH)
        va, var_ = vaR.next()
        vb, vbr = vbR.next()
        for tt in range(4):
            for which in range(2):
                c0 = 1024 if which == 0 else 2560
                pv, pvr = pjR.next()
                for kc in range(8):
                    mm(pv[:], hT[sl][:, kc, tt * 128:(tt + 1) * 128], w_in_bf[:, kc, c0:c0 + 512], kc == 0, kc == 7,
                       [r_w, r_hT[sl]], [pvr])
                if pend_qk:
                    pend_qk.pop()()
                if which == 0:
                    for hh in range(2):
                        P.add("dve", lambda e, pv=pv, tt=tt, hh=hh: e.tensor_copy(
                            out=va[:, :, tt, 65 * hh:65 * hh + 64],
                            in_=pv[:].rearrange("p (r h d) -> p r h d", h=2, d=64)[:, :, hh, :]),
                            reads=[pvr], writes=[var_])
                else:
                    P.add("act", lambda e, pv=pv, tt=tt: e.activation(
                        out=vb[:, :, tt, 0:128], in_=pv[:].rearrange("p (h d) -> p h d", d=128), func=AF.Copy),
                        reads=[pvr], writes=[vbr])
        for pr_ in range(4):
            P.add("sp", lambda e, pr_=pr_: e.dma_start(
                out=VA[b, pr_, cc * 512:(cc + 1) * 512, :].rearrange("(t p) c -> p t c", p=128),
                in_=va[:, pr_, :, :]), reads=[var_], writes=[], dma=True,
                semkey=f"vast{vaR.i % 2}")
        P.add("sp", lambda e: e.dma_start(
            out=VB[b, :, :, cc * 4:(cc + 1) * 4, :].rearrange("h p t c -> p h (t c)"),
            in_=vb[:].rearrange("p h t c -> p h (t c)")), reads=[vbr], writes=[], dma=True,
            semkey=f"vbst{vbR.i % 2}")

    def s1_g(ci):
        sl = ci % 2
        b, cc = divmod(ci, NCH)
        for grp in range(4):
            st, str_ = gR.next()
            for m in range(4):
                c0 = 3072 + (grp * 4 + m) * 128
                pg, pgr = pjR.next()
                for kc in range(8):
                    mm(pg[:], w_in_bf[:, kc, c0:c0 + 128], hT[sl][:, kc, :], kc == 0, kc == 7,
                       [r_w, r_hT[sl]], [pgr])
                P.add("act", lambda e, pg=pg, st=st, m=m: e.activation(out=st[:, m, :], in_=pg[:], func=AF.Sigmoid),
                      reads=[pgr], writes=[str_])
            P.add("sp", lambda e, st=st, grp=grp: e.dma_start(
                out=GT[b, grp * 512:(grp + 1) * 512, cc * 512:(cc + 1) * 512].rearrange("(m p) s -> p m s", p=128),
                in_=st[:]), reads=[str_], writes=[], dma=True, semkey=f"gst{gR.i % 2}")

    r_scr = P.res("scratch")
    NCI = NB * NCH
    if stop_after == 0:
        P.fence()
        P.add("sp", None)
        P.emit()
        return nc
    if stop_after is not None and 0 < stop_after < 1:
        NCI = 1
    s1_load(0)
    s1_norm(0)
    for ci in range(NCI):
        if stop_after == 0.1:
            break
        if ci + 1 < NCI:
            s1_load(ci + 1)
        pend_qk = s1_qk(ci)
        if stop_after == 0.2:
            break
        if ci + 1 < NCI:
            s1_norm(ci + 1)
        s1_v(ci, pend_qk)
        if stop_after == 0.3:
            break
        s1_g(ci)
    P.fence()
    if stop_after is not None and stop_after <= 1:
        P.add("sp", None)
        P.emit()
        return nc

    A = Alloc(nc, "s2")
    idb2 = A.t("idb", [128, 128], BF16)
    lam_sb = A.t("lamv", [128, 4, 64], F32)
    lam_t = A.t("lamt", [128, 8], F32)
    subg_sb = A.t("subg", [128, 128], F32)
    kt_sb = [A.t(f"kt{i}", [128, S], BF16) for i in range(2)]
    qt_sb = [A.t(f"qt{i}", [128, S], BF16) for i in range(2)]
    v_sb = [A.t(f"v{i}", [128, 32, 130], BF16) for i in range(2)]
    NE = 8
    e_sb = [A.t(f"e{i}", [128, 512], BF16) for i in range(NE)]
    o_sb = A.t("o", [128, 128], F32)
    o2_sb = A.t("o2", [128, 128], F32)
    ojunk = A.t("ojunk", [128, 128], F32)
    sm = A.t("sm", [128, 8], F32)
    ob = [A.t(f"ob{i}", [128, 128], BF16) for i in range(8)]
    acc_sb = [A.t(f"accsb{i}", [128, 8, 129], F32) for i in range(2)]
    ot_st = [A.t(f"otst{i}", [128, 512], BF16) for i in range(2)]

    r_c2 = P.res("const2")
    r_kt = [P.res() for _ in range(2)]
    r_qt = [P.res() for _ in range(2)]
    r_v = [P.res() for _ in range(2)]
    eR = Rot([(e_sb[i], P.res()) for i in range(NE)])
    r_o, r_o2, r_oj, r_sm = P.res(), P.res(), P.res(), P.res()
    obR = Rot([(ob[i], P.res()) for i in range(8)])
    accR = Rot([(acc_sb[i], P.res()) for i in range(2)])
    otR = Rot([(ot_st[i], P.res()) for i in range(2)])
    psS = Rot([(ps[i], psr[i]) for i in range(4)])
    for dst, src, k in ((idb2, ident, "c2_id"), (lam_sb, lamv, "c2_lam"), (subg_sb, subg, "c2_sg")):
        P.add("sp", lambda e, dst=dst, src=src: e.dma_start(out=dst[:], in_=src), writes=[r_c2], dma=True, semkey=k)
    P.add("dve", lambda e: e.tensor_tensor(out=lam_sb[:, 0, :], in0=lam_sb[:, 0, :], in1=lam_sb[:, 1, :], op=ALU.mult),
          reads=[r_c2], writes=[r_c2])
    P.add("dve", lambda e: e.tensor_tensor(out=lam_sb[:, 2, :], in0=lam_sb[:, 2, :], in1=lam_sb[:, 3, :], op=ALU.mult),
          reads=[r_c2], writes=[r_c2])
    P.add("dve", lambda e: e.tensor_reduce(out=lam_t[:, 0:1], in_=lam_sb[:, 0, :], axis=mybir.AxisListType.X, op=ALU.add),
          reads=[r_c2], writes=[r_c2])
    P.add("dve", lambda e: e.tensor_reduce(out=lam_t[:, 1:2], in_=lam_sb[:, 2, :], axis=mybir.AxisListType.X, op=ALU.add),
          reads=[r_c2], writes=[r_c2])
    P.add("act", lambda e: e.activation(out=lam_t[:, 0:2], in_=lam_t[:, 0:2], func=AF.Exp), reads=[r_c2], writes=[r_c2])
    P.add("dve", lambda e: e.scalar_tensor_tensor(out=lam_t[:, 2:3], in0=lam_t[:, 1:2], scalar=-0.2, in1=lam_t[:, 0:1],
                                                  op0=ALU.add, op1=ALU.subtract), reads=[r_c2], writes=[r_c2])
    P.add("dve", lambda e: e.tensor_scalar(out=subg_sb[:], in0=subg_sb[:], scalar1=0.8, scalar2=None, op0=ALU.mult),
          reads=[r_c2], writes=[r_c2])

    def acc_ap(i, width):
        bank = 4 + i // 3
        slot = i % 3
        return ps[bank][:, slot * 129: slot * 129 + width], psr[bank]

    def attn_pass(b, kt, qt, vt, r_in, k_rows, q_rows, nmaps, vcol0, vw, j, masked):
        if masked:
            kts = list(range(max(0, 4 * j - 8), min(31, 4 * j + 11) + 1))
        else:
            kts = list(range(32))
        accs = [[acc_ap(c * 4 + q4, vw) for q4 in range(4)] for c in range(nmaps)]
        pend = []

        def issue_qk(kti):
            lst = []
            for c in range(nmaps):
                pS, pSr = psS.next()
                mm(pS[:], kt[k_rows[c]:k_rows[c] + 64, kti * 128:(kti + 1) * 128],
                   qt[q_rows[c]:q_rows[c] + 64, j * 512:(j + 1) * 512], True, True, r_in, [pSr])
                eb, ebr = eR.next()
                P.add("act", lambda e, eb=eb, pS=pS: e.activation(out=eb[:], in_=pS[:], func=AF.Exp, scale=0.125),
                      reads=[pSr], writes=[ebr])
                if masked:
                    mi = kti - 4 * j + 8
                    P.add("dve", lambda e, eb=eb, mi=mi: e.tensor_tensor(out=eb[:], in0=eb[:], in1=mask_sb[:, mi, :],
                                                                         op=ALU.mult), reads=[ebr, r_c2], writes=[ebr])
                lst.append((eb, ebr))
            return lst

        started = set()

        def issue_pv(kti, lst, first, last):
            for c in range(nmaps):
                eb, ebr = lst[c]
                for q4 in range(4):
                    ap_, rr = accs[c][q4]
                    st_ = first and id(rr) not in started
                    started.add(id(rr))
                    mm(ap_, eb[:, q4 * 128:(q4 + 1) * 128], vt[:, kti, vcol0:vcol0 + vw], st_, last,
                       [ebr] + r_in, [rr])

        def issue_qk1(kti, c):
            pS, pSr = psS.next()
            mm(pS[:], kt[k_rows[c]:k_rows[c] + 64, kti * 128:(kti + 1) * 128],
               qt[q_rows[c]:q_rows[c] + 64, j * 512:(j + 1) * 512], True, True, r_in, [pSr])
            eb, ebr = eR.next()
            P.add("act", lambda e, eb=eb, pS=pS: e.activation(out=eb[:], in_=pS[:], func=AF.Exp, scale=0.125),
                  reads=[pSr], writes=[ebr])
            return eb, ebr

        def issue_pv1(kti, c, eb, ebr, first, last):
            for q4 in range(4):
                ap_, rr = accs[c][q4]
                st_ = first and id(rr) not in started
                started.add(id(rr))
                mm(ap_, eb[:, q4 * 128:(q4 + 1) * 128], vt[:, kti, vcol0:vcol0 + vw], st_, last, [ebr] + r_in, [rr])

        nk = len(kts)
        ebs = {}
        for t_ in range(min(2, nk)):
            for c in range(nmaps):
                ebs[(t_, c)] = issue_qk1(kts[t_], c)
        for t_ in range(nk):
            for c in range(nmaps):
                eb, ebr = ebs.pop((t_, c))
                issue_pv1(kts[t_], c, eb, ebr, t_ == 0, t_ == nk - 1)
                if t_ + 2 < nk:
                    ebs[(t_ + 2, c)] = issue_qk1(kts[t_ + 2], c)
        asb, asr = accR.next()
        nb_ = (nmaps * 4 + 2) // 3
        for k_ in range(nb_):
            w_ = min(3, nmaps * 4 - 3 * k_)
            P.add("dve", lambda e, k_=k_, w_=w_: e.tensor_copy(
                out=asb[:, 3 * k_:3 * k_ + w_, :].rearrange("p a b -> p (a b)"), in_=ps[4 + k_][:, 0:129 * w_]),
                reads=[psr[4 + k_]], writes=[asr])
        return [[(asb[:, c * 4 + q4, :], asr) for q4 in range(4)] for c in range(nmaps)]

    def load_b(b, h, sl):
        P.add("sp", lambda e: e.dma_start(out=kt_sb[sl][:], in_=QKT[b, 1536 + 128 * h:1536 + 128 * (h + 1), :]),
              writes=[r_kt[sl]], dma=True, semkey=f"kt{sl}")
        P.add("sp", lambda e: e.dma_start(out=qt_sb[sl][:], in_=QKT[b, 1024 + 128 * h:1024 + 128 * (h + 1), :]),
              writes=[r_qt[sl]], dma=True, semkey=f"qt{sl}")
        P.add("sp", lambda e: e.dma_start(
            out=v_sb[sl][:, :, 0:129], in_=VB[b, h]),
            writes=[r_v[sl]], dma=True, semkey=f"v{sl}")

    bh = [(b, h) for b in range(NB) for h in range(4)]
    load_b(0, 0, 0)
    deferred_b = []
    for n, (b, h) in enumerate(bh):
        sl = n % 2
        if n + 1 < len(bh):
            load_b(bh[n + 1][0], bh[n + 1][1], (n + 1) % 2)
        r_in = [r_kt[sl], r_qt[sl], r_v[sl]]
        for j in range(NCH):
            accs = attn_pass(b, kt_sb[sl], qt_sb[sl], v_sb[sl], r_in, [0, 64], [0, 64], 2, 0, 129, j, False)
            while deferred_b:
                deferred_b.pop(0)()
            st, str_ = otR.next()
            tr_list = []
            for q4 in range(4):
                a1, a1r = accs[0][q4]
                a2, a2r = accs[1][q4]
                P.add("dve", lambda e, a1=a1: e.reciprocal(out=sm[:, 0:1], in_=a1[:, 128:129]), reads=[a1r], writes=[r_sm])
                P.add("dve", lambda e, a2=a2: e.reciprocal(out=sm[:, 1:2], in_=a2[:, 128:129]), reads=[a2r], writes=[r_sm])
                P.add("dve", lambda e: e.tensor_tensor(out=sm[:, 2:3], in0=sm[:, 1:2], in1=lam_t[:, 2:3], op=ALU.mult),
                      reads=[r_sm, r_c2], writes=[r_sm])
                P.add("dve", lambda e, a1=a1: e.tensor_scalar(out=o_sb[:], in0=a1[:, 0:128], scalar1=sm[:, 0:1],
                                                              scalar2=None, op0=ALU.mult), reads=[a1r, r_sm], writes=[r_o])
                P.add("dve", lambda e, a2=a2: e.scalar_tensor_tensor(out=o2_sb[:], in0=a2[:, 0:128], scalar=sm[:, 2:3],
                                                                    in1=o_sb[:], op0=ALU.mult, op1=ALU.add),
                      reads=[a2r, r_sm, r_o], writes=[r_o2])
                P.add("dve", lambda e: e.tensor_tensor(out=ojunk[:], in0=o2_sb[:], in1=o2_sb[:], op=ALU.mult),
                      reads=[r_o2], writes=[r_oj])
                P.add("dve", lambda e: e.tensor_reduce(out=sm[:, 3:4], in_=ojunk[:], axis=mybir.AxisListType.X, op=ALU.add),
                      reads=[r_oj], writes=[r_sm])
                P.add("dve", lambda e: e.tensor_scalar(out=sm[:, 3:4], in0=sm[:, 3:4], scalar1=1.0 / 128, scalar2=1e-5,
                                                       op0=ALU.mult, op1=ALU.add), reads=[r_sm], writes=[r_sm])
                P.add("act", lambda e: e.activation(out=sm[:, 4:5], in_=sm[:, 3:4], func=AF.Ln), reads=[r_sm], writes=[r_sm])
                P.add("act", lambda e: e.activation(out=sm[:, 4:5], in_=sm[:, 4:5], func=AF.Exp, scale=-0.5),
                      reads=[r_sm], writes=[r_sm])
                obb, obr = obR.next()
                P.add("dve", lambda e, obb=obb: e.scalar_tensor_tensor(out=obb[:], in0=o2_sb[:], scalar=sm[:, 4:5],
                                                                       in1=subg_sb[:], op0=ALU.mult, op1=ALU.mult),
                      reads=[r_o2, r_sm, r_c2], writes=[obr])
                tr_list.append((obb, obr, q4))

            def fin_b(tr_list=tr_list, st=st, str_=str_, b=b, h=h, j=j, key=f"otst{otR.i % 2}"):
                ptb = ps[7].bitcast(BF16)
                for (obb, obr, q4) in tr_list:
                    P.add("pe", lambda e, obb=obb, q4=q4: e.transpose(out=ptb[:, q4 * 128:(q4 + 1) * 128], in_=obb[:],
                                                                       identity=idb2[:]), reads=[obr, r_c2], writes=[psr[7]])
                P.add("dve", lambda e: e.tensor_copy(out=st[:], in_=ptb[:, 0:512]), reads=[psr[7]], writes=[str_])
                P.add("sp", lambda e: e.dma_start(
                    out=OT[b, 512 + 128 * h:512 + 128 * (h + 1), j * 512:(j + 1) * 512], in_=st[:]),
                    reads=[str_], writes=[], dma=True, semkey=key)
            deferred_b.append(fin_b)

    while deferred_b:
        deferred_b.pop(0)()
    P.fence()
    if stop_after == 1.5:
        P.add("sp", None)
        P.emit()
        return nc

    A = Alloc(nc, "s2a")
    PATS = (1, 4, 16)
    m2_sb = A.t("m2", [128, 2, 256], BF16)
    ones_f = A.t("onesf", [128, 64], F32)
    ida = A.t("ida", [128, 128], BF16)
    ktp = [[A.t(f"ktp{i}_{g}", [128, S + 128], BF16) for g in PATS] for i in range(2)]
    qtp = [[A.t(f"qtp{i}_{g}", [128, S], BF16) for g in PATS] for i in range(2)]
    vh = [[A.t(f"vh{i}_{g}", [128, 33, 130], BF16) for g in PATS] for i in range(2)]
    acc = [A.t(f"acc{i}", [128, S], F32) for i in range(2)]
    NE2 = 8
    e2 = [A.t(f"e{i}", [128, 256], BF16) for i in range(NE2)]
    oa_st = [A.t(f"oast{i}", [128, 512], BF16) for i in range(2)]
    rz_sb = [A.t(f"rz{i}", [128, 512], F32) for i in range(2)]
    r_c2a = P.res()
    r_ktp = [[P.res() for _ in PATS] for _ in range(2)]
    r_qtp = [[P.res() for _ in PATS] for _ in range(2)]
    r_vh = [[P.res() for _ in PATS] for _ in range(2)]
    r_acc = [P.res() for _ in range(2)]
    e2R = Rot([(e2[i], P.res()) for i in range(NE2)])
    oaR = Rot([(oa_st[i], P.res()) for i in range(2)])
    rzR = Rot([(rz_sb[i], P.res()) for i in range(2)])
    psS2 = Rot([(ps[i], psr[i]) for i in range(6)])
    psO2 = Rot([(ps[6 + i], psr[6 + i]) for i in range(2)])
    P.add("sp", lambda e: e.dma_start(out=m2_sb[:], in_=mask2), writes=[r_c2a], dma=True, semkey="c2a_m")
    P.add("sp", lambda e: e.dma_start(out=ida[:], in_=ident), writes=[r_c2a], dma=True, semkey="c2a_i")
    P.add("pool", lambda e: e.memset(ones_f[:], 1.0), writes=[r_c2a])
    for i in range(2):
        for gi in range(3):
            P.add("pool", lambda e, i=i, gi=gi: e.memset(ktp[i][gi][:, 0:64], 0.0), writes=[r_ktp[i][gi]])
            P.add("pool", lambda e, i=i, gi=gi: e.memset(ktp[i][gi][:, S + 64:S + 128], 0.0), writes=[r_ktp[i][gi]])
            P.add("pool", lambda e, i=i, gi=gi: e.memset(vh[i][gi][0:64, 0, :], 0.0), writes=[r_vh[i][gi]])
            P.add("pool", lambda e, i=i, gi=gi: e.memset(vh[i][gi][64:128, 32, :], 0.0), writes=[r_vh[i][gi]])

    def a_load(b, pr, sl):
        P.add("sp", lambda e: e.dma_start(out=ktp[sl][0][:, 64:64 + S], in_=QKT[b, 512 + 128 * pr:512 + 128 * (pr + 1), :]),
              writes=[r_ktp[sl][0]], dma=True, semkey=f"a_kt{sl}")
        P.add("sp", lambda e: e.dma_start(out=qtp[sl][0][:], in_=QKT[b, 128 * pr:128 * (pr + 1), :]),
              writes=[r_qtp[sl][0]], dma=True, semkey=f"a_qt{sl}")
        for gi, g in enumerate(PATS):
            n_ = S // g // 128
            if gi > 0:
                P.add("pool", lambda e, gi=gi, g=g: e.tensor_copy(
                    out=ktp[sl][gi][:, 64:64 + S].rearrange("p (r i) -> p r i", r=g),
                    in_=ktp[sl][0][:, 64:64 + S].rearrange("p (i r) -> p r i", r=g)),
                    reads=[r_ktp[sl][0]], writes=[r_ktp[sl][gi]])
                P.add("pool", lambda e, gi=gi, g=g: e.tensor_copy(
                    out=qtp[sl][gi][:].rearrange("p (r i) -> p r i", r=g),
                    in_=qtp[sl][0][:].rearrange("p (i r) -> p r i", r=g)),
                    reads=[r_qtp[sl][0]], writes=[r_qtp[sl][gi]])
            src = VA[b, pr].rearrange("(m j r) c -> r j m c", j=128, r=g)
            for sg in range(g):
                for m0 in range(0, n_, 8):
                    m1 = min(n_, m0 + 8)
                    P.add("sp", lambda e, gi=gi, sg=sg, n_=n_, src=src, m0=m0, m1=m1: e.dma_start(
                        out=vh[sl][gi][64:128, sg * n_ + m0:sg * n_ + m1, :], in_=src[sg, 0:64, m0:m1, :]),
                        writes=[r_vh[sl][gi]], dma=True, semkey=f"a_v{sl}_{gi}")
                    P.add("sp", lambda e, gi=gi, sg=sg, n_=n_, src=src, m0=m0, m1=m1: e.dma_start(
                        out=vh[sl][gi][0:64, sg * n_ + 1 + m0:sg * n_ + 1 + m1, :], in_=src[sg, 64:128, m0:m1, :]),
                        writes=[r_vh[sl][gi]], dma=True, semkey=f"a_v{sl}_{gi}")

    def a_head(b, pr, sl, hh, ai):
        ac, acr = acc[ai], r_acc[ai]
        for gi, g in enumerate(PATS):
            L = S // g
            n_ = L // 128
            tiles = []
            for u in range(33):
                if u == 0:
                    tiles.append((u, 0, 128, 1, 128))
                elif u == 32:
                    tiles.append((u, 128 * 31, 128, 1, 0))
                elif u % n_ == 0:
                    tiles.append((u, 128 * (u - 1), 256, 1, 0))
                else:
                    tiles.append((u, 128 * (u - 1), 256, 0, 0))
            banks = {}

            def qk(tl):
                u, q0, N, mk, mc0 = tl
                pS, pSr = psS2.next()
                mm(pS[:, 0:N], ktp[sl][gi][64 * hh:64 * hh + 64, 128 * u:128 * u + 128],
                   qtp[sl][gi][64 * hh:64 * hh + 64, q0:q0 + N], True, False, [r_ktp[sl][gi], r_qtp[sl][gi]], [pSr])
                mm(pS[:, 0:N], ida[:], m2_sb[:, mk, mc0:mc0 + N], False, True, [r_c2a], [pSr])
                eb, ebr = e2R.next()
                P.add("act", lambda e: e.activation(out=eb[:, 0:N], in_=pS[:, 0:N], func=AF.Exp, scale=0.125),
                      reads=[pSr], writes=[ebr])
                return eb, ebr

            def evac(w, po, por):
                if g == 1:
                    P.add("act", lambda e: e.activation(out=ac[0:65, 512 * w:512 * w + 512], in_=po[0:65, :], func=AF.Copy),
                          reads=[por], writes=[acr])
                    return
                pieces = [(512 * w // L, 512 * w % L, 0, 512)] if L >= 512 else \
                    [(2 * w, 0, 0, 256), (2 * w + 1, 0, 256, 256)]
                for (sg, i0_, c0, cnt) in pieces:
                    P.add("dve", lambda e, sg=sg, i0_=i0_, c0=c0, cnt=cnt, g=g: e.tensor_tensor(
                        out=ac[0:65, :].rearrange("p (i r) -> p r i", r=g)[:, sg, i0_:i0_ + cnt],
                        in0=po[0:65, c0:c0 + cnt],
                        in1=ac[0:65, :].rearrange("p (i r) -> p r i", r=g)[:, sg, i0_:i0_ + cnt], op=ALU.add),
                        reads=[por, acr], writes=[acr])

            def pv(tl, eb, ebr):
                u, q0, N, mk, mc0 = tl
                halves = []
                if u >= 1:
                    halves.append((u - 1, 0))
                if u <= 31:
                    halves.append((u, 0 if u == 0 else 128))
                for (v, c0) in halves:
                    w = v // 4
                    if w not in banks:
                        banks[w] = psO2.next()
                    po, por = banks[w]
                    opening = (v == u)
                    mm(po[0:65, (v % 4) * 128:(v % 4) * 128 + 128], vh[sl][gi][:, u, 65 * hh:65 * hh + 65],
                       eb[:, c0:c0 + 128], opening and v % 4 == 0, not opening, [ebr, r_vh[sl][gi]], [por])
                    if (not opening) and v % 4 == 3:
                        evac(w, po, por)

            queue = []
            for tl in tiles:
                queue.append((tl, qk(tl)))
                if len(queue) > 5:
                    t_, c_ = queue.pop(0)
                    pv(t_, *c_)
            for t_, c_ in queue:
                pv(t_, *c_)
        for j in range(NCH):
            pz, pzr = psO2.next()
            mm(pz[0:64, :], ones_f[64:65, 0:64], ac[64:65, j * 512:(j + 1) * 512], True, True, [acr, r_c2a], [pzr])
            rzb, rzr = rzR.next()
            P.add("dve", lambda e, pz=pz, rzb=rzb: e.reciprocal(out=rzb[0:64, :], in_=pz[0:64, :]), reads=[pzr], writes=[rzr])
            st, str_ = oaR.next()
            P.add("dve", lambda e, rzb=rzb, st=st, j=j: e.tensor_tensor(out=st[0:64, :], in0=rzb[0:64, :],
                                                                       in1=ac[0:64, j * 512:(j + 1) * 512], op=ALU.mult),
                  reads=[rzr, acr], writes=[str_])
            P.add("sp", lambda e, st=st, j=j: e.dma_start(
                out=OT[b, 128 * pr + 64 * hh:128 * pr + 64 * hh + 64, j * 512:(j + 1) * 512], in_=st[0:64, :]),
                reads=[str_], writes=[], dma=True, semkey=f"oast{oaR.i % 2}")

    bp = [(b, pr) for b in range(NB) for pr in range(4)]
    a_load(0, 0, 0)
    for n, (b, pr) in enumerate(bp):
        if n + 1 < len(bp):
            a_load(bp[n + 1][0], bp[n + 1][1], (n + 1) % 2)
        for hh in range(2):
            a_head(b, pr, n % 2, hh, hh)
    P.fence()
    if stop_after == 2:
        P.add("sp", None)
        P.emit()
        return nc

    A = Alloc(nc, "s3")
    idb3 = A.t("idb", [128, 128], BF16)
    wab = A.t("wab", [128, 8, D], BF16)
    wo = A.t("wo", [128, 8, D], BF16)
    ot_in = [A.t(f"otin{i}", [128, 8, 512], BF16) for i in range(2)]
    gt_in = [A.t(f"gtin{i}", [128, 16, 512], BF16) for i in range(2)]
    x_in = [A.t(f"xin{i}", [128, 4, D], F32) for i in range(2)]
    u1 = [A.t(f"u1{i}", [128, 512], F32) for i in range(2)]
    u2 = [A.t(f"u2{i}", [128, 512], F32) for i in range(2)]
    mT = A.t("mT", [128, 8, 512], BF16)
    junk3 = A.t("junk", [128, D], BF16)
    ss3 = A.t("ss", [128, 4], F32)
    rs3 = A.t("rs", [128, 4], F32)
    h2 = A.t("h2", [128, 4, D], BF16)
    h2t_st = [A.t(f"h2t{i}", [128, 8, 512], BF16) for i in range(2)]
    r_c3 = P.res("const3")
    r_otin = [P.res() for _ in range(2)]
    r_gtin = [P.res() for _ in range(2)]
    r_x3 = [P.res() for _ in range(2)]
    u1R = Rot([(u1[i], P.res()) for i in range(2)])
    u2R = Rot([(u2[i], P.res()) for i in range(2)])
    r_mT, r_j3, r_ss3, r_rs3, r_h2 = P.res(), P.res(), P.res(), P.res(), P.res()
    h2R = Rot([(h2t_st[i], P.res()) for i in range(2)])
    pA = Rot([(ps[0], psr[0]), (ps[1], psr[1])])
    pB = Rot([(ps[2], psr[2]), (ps[3], psr[3])])
    pO = Rot([(ps[4], psr[4]), (ps[5], psr[5])])
    pT3 = Rot([(ps[6], psr[6]), (ps[7], psr[7])])
    P.add("sp", lambda e: e.dma_start(out=idb3[:], in_=ident), writes=[r_c3], dma=True, semkey="c3_id")
    P.add("pool", lambda e: e.dma_start(out=wab[:, 0:4, :], in_=w_a.rearrange("(c p) n -> p c n", p=128)),
          writes=[r_c3], dma=True, semkey="c3_wa")
    P.add("pool", lambda e: e.dma_start(out=wab[:, 4:8, :], in_=w_b.rearrange("(c p) n -> p c n", p=128)),
          writes=[r_c3], dma=True, semkey="c3_wb")
    P.add("pool", lambda e: e.dma_start(out=wo[:], in_=w_out.rearrange("(c p) n -> p c n", p=128)),
          writes=[r_c3], dma=True, semkey="c3_wo")

    def s3_load(ci):
        sl = ci % 2
        b, cc = divmod(ci, NCH)
        P.add("sp", lambda e: e.dma_start(out=ot_in[sl][:], in_=OT[b, :, cc * 512:(cc + 1) * 512].rearrange(
            "(c p) s -> p c s", p=128)), writes=[r_otin[sl]], dma=True, semkey=f"otin{sl}")
        P.add("sp", lambda e: e.dma_start(out=gt_in[sl][:], in_=GT[b, :, cc * 512:(cc + 1) * 512].rearrange(
            "(c p) s -> p c s", p=128)), writes=[r_gtin[sl]], dma=True, semkey=f"gtin{sl}")
        P.add("sp", lambda e: e.dma_start(out=x_in[sl][:], in_=x[ci * 512:(ci + 1) * 512, :].rearrange(
            "(t p) f -> p t f", p=128)), writes=[r_x3[sl]], dma=True, semkey=f"x3{sl}")

    def s3_chunk(ci):
        sl = ci % 2
        b, cc = divmod(ci, NCH)
        if ci + 1 < NCI:
            s3_load(ci + 1)
        for m in range(8):
            pa, par = pA.next()
            pb_, pbr = pB.next()
            for kc in range(4):
                mm(pa[:], wab[:, kc, m * 128:(m + 1) * 128], ot_in[sl][:, kc, :], kc == 0, kc == 3, [r_c3, r_otin[sl]], [par])
            for kc in range(4):
                mm(pb_[:], wab[:, 4 + kc, m * 128:(m + 1) * 128], ot_in[sl][:, 4 + kc, :], kc == 0, kc == 3,
                   [r_c3, r_otin[sl]], [pbr])
            a1, a1r = u1R.next()
            a2, a2r = u2R.next()
            P.add("dve", lambda e, a1=a1, pa=pa, m=m: e.tensor_tensor(out=a1[:], in0=pa[:], in1=gt_in[sl][:, m, :], op=ALU.mult),
                  reads=[par, r_gtin[sl]], writes=[a1r])
            P.add("dve", lambda e, a2=a2, pb_=pb_, m=m: e.tensor_tensor(out=a2[:], in0=pb_[:], in1=gt_in[sl][:, 8 + m, :],
                                                                       op=ALU.mult), reads=[pbr, r_gtin[sl]], writes=[a2r])
            P.add("pool", lambda e, a1=a1, a2=a2, m=m: e.tensor_tensor(out=mT[:, m, :], in0=a1[:], in1=a2[:], op=ALU.add),
                  reads=[a1r, a2r], writes=[r_mT])
        for tt in range(4):
            for c2 in range(2):
                po, por = pO.next()
                for kc in range(8):
                    mm(po[:], mT[:, kc, tt * 128:(tt + 1) * 128], wo[:, kc, c2 * 512:(c2 + 1) * 512], kc == 0, kc == 7,
                       [r_mT, r_c3], [por])
                P.add("dve", lambda e, po=po, tt=tt, c2=c2: e.tensor_tensor(
                    out=x_in[sl][:, tt, c2 * 512:(c2 + 1) * 512], in0=po[:], in1=x_in[sl][:, tt, c2 * 512:(c2 + 1) * 512],
                    op=ALU.add), reads=[por, r_x3[sl]], writes=[r_x3[sl]])
        P.add("sp", lambda e, ci=ci: e.dma_start(out=X1[ci * 512:(ci + 1) * 512, :].rearrange("(t p) f -> p t f", p=128),
                                                 in_=x_in[sl][:]), reads=[r_x3[sl]], writes=[], dma=True, semkey=f"x1st{sl}")
        for tt in range(4):
            P.add("act", lambda e, tt=tt: e.activation(out=junk3[:], in_=x_in[sl][:, tt, :], func=AF.Square,
                                                       accum_out=ss3[:, tt:tt + 1]), reads=[r_x3[sl]], writes=[r_j3, r_ss3])
        P.add("dve", lambda e: e.tensor_scalar(out=rs3[:], in0=ss3[:], scalar1=1.0 / D, scalar2=1e-6, op0=ALU.mult, op1=ALU.add),
              reads=[r_ss3], writes=[r_rs3])
        P.add("act", lambda e: e.activation(out=rs3[:], in_=rs3[:], func=AF.Ln), reads=[r_rs3], writes=[r_rs3])
        P.add("act", lambda e: e.activation(out=rs3[:], in_=rs3[:], func=AF.Exp, scale=-0.5), reads=[r_rs3], writes=[r_rs3])
        for tt in range(4):
            P.add("act", lambda e, tt=tt: e.activation(out=h2[:, tt, :], in_=x_in[sl][:, tt, :], func=AF.Copy,
                                                       scale=rs3[:, tt:tt + 1]), reads=[r_x3[sl], r_rs3], writes=[r_h2])
        st, str_ = h2R.next()
        for fc in range(8):
            pt, ptr = pT3.next()
            ptb = pt.bitcast(BF16)
            for tt in range(4):
                P.add("pe", lambda e, tt=tt, fc=fc, ptb=ptb: e.transpose(
                    out=ptb[:, tt * 128:(tt + 1) * 128], in_=h2[:, tt, fc * 128:(fc + 1) * 128], identity=idb3[:]),
                    reads=[r_h2, r_c3], writes=[ptr])
            P.add("dve", lambda e, fc=fc, ptb=ptb, st=st: e.tensor_copy(out=st[:, fc, :], in_=ptb[:, 0:512]),
                  reads=[ptr], writes=[str_])
        P.add("sp", lambda e, st=st, b=b, cc=cc: e.dma_start(
            out=H2T[b, :, cc * 512:(cc + 1) * 512].rearrange("(c p) s -> p c s", p=128), in_=st[:]),
            reads=[str_], writes=[], dma=True, semkey=f"h2st{h2R.i % 2}")

    s3_load(0)
    for ci in range(NCI):
        s3_chunk(ci)
    P.fence()
    if stop_after == 3:
        P.add("sp", None)
        P.emit()
        return nc

    A = Alloc(nc, "s4")
    w1 = A.t("w1", [128, 8, DFF], BF16)
    w2 = A.t("w2", [128, 32, D], BF16)
    rl = [A.t(f"rl{i}", [128, 512], F32) for i in range(2)]
    gmlp_sb = A.t("gmlp", [128, 8], F32)
    gfin_sb = A.t("gfin", [128, D], F32)
    h2t_in = [A.t(f"h2tin{i}", [128, 8, 512], BF16) for i in range(2)]
    x1_in = A.t("x1in", [128, 4, D], F32)
    uT = A.t("uT", [128, 32, 512], BF16)
    junk4 = A.t("junk", [128, D], BF16)
    ss4 = A.t("ss", [128, 4], F32)
    rs4 = A.t("rs", [128, 4], F32)
    r_c4 = P.res("const4")
    r_w1, r_w2 = P.res(), P.res()
    rlR = Rot([(rl[i], P.res()) for i in range(2)])
    r_h2in = [P.res() for _ in range(2)]
    r_x1 = P.res()
    r_uT, r_j4, r_ss4, r_rs4 = P.res(), P.res(), P.res(), P.res()
    pF = Rot([(ps[i], psr[i]) for i in range(4)])
    pG = Rot([(ps[4 + i], psr[4 + i]) for i in range(4)])
    P.add("sp", lambda e: e.dma_start(out=gmlp_sb[:], in_=gmlp), writes=[r_c4], dma=True, semkey="c4_g")
    P.add("sp", lambda e: e.dma_start(out=gfin_sb[:], in_=gfin), writes=[r_c4], dma=True, semkey="c4_gf")
    P.add("pool", lambda e: e.dma_start(out=w2[:], in_=w_ff2.rearrange("(c p) n -> p c n", p=128)),
          writes=[r_w2], dma=True, semkey="c4_w2")
    r_w1k = [P.res() for _ in range(8)]
    for kc in range(8):
        P.add("pool", lambda e, kc=kc: e.dma_start(out=w1[:, kc, :], in_=w_ff1[kc * 128:(kc + 1) * 128, :]),
              writes=[r_w1k[kc]], dma=True, semkey=f"w1_{kc}")
        if kc % 2:
            P.add("dve", lambda e, kc=kc: e.tensor_scalar_mul(out=w1[:, kc, :], in0=w1[:, kc, :], scalar1=gmlp_sb[:, kc:kc + 1]),
                  reads=[r_w1k[kc], r_c4], writes=[r_w1k[kc], r_w1])
        else:
            P.add("act", lambda e, kc=kc: e.activation(out=w1[:, kc, :], in_=w1[:, kc, :], func=AF.Copy,
                                                        scale=gmlp_sb[:, kc:kc + 1]),
                  reads=[r_w1k[kc], r_c4], writes=[r_w1k[kc], r_w1])

    def s4_load_h(ci):
        sl = ci % 2
        b, cc = divmod(ci, NCH)
        P.add("sp", lambda e: e.dma_start(out=h2t_in[sl][:], in_=H2T[b, :, cc * 512:(cc + 1) * 512].rearrange(
            "(c p) s -> p c s", p=128)), writes=[r_h2in[sl]], dma=True, semkey=f"h2in{sl}")

    def s4_chunk(ci):
        sl = ci % 2
        if ci + 1 < NCI:
            s4_load_h(ci + 1)
        P.add("sp", lambda e, ci=ci: e.dma_start(out=x1_in[:], in_=X1[ci * 512:(ci + 1) * 512, :].rearrange(
            "(t p) f -> p t f", p=128)), writes=[r_x1], dma=True, semkey="x1in")
        for m in range(32):
            pf, pfr = pF.next()
            for kc in range(8):
                mm(pf[:], w1[:, kc, m * 128:(m + 1) * 128], h2t_in[sl][:, kc, :], kc == 0, kc == 7, [r_w1, r_h2in[sl]], [pfr])
            rb, rbr = rlR.next()
            P.add("act", lambda e, pf=pf, rb=rb: e.activation(out=rb[:], in_=pf[:], func=AF.Relu), reads=[pfr], writes=[rbr])
            P.add("dve" if m % 2 else "pool", lambda e, rb=rb, m=m: e.tensor_tensor(out=uT[:, m, :], in0=rb[:], in1=rb[:], op=ALU.mult),
                  reads=[rbr], writes=[r_uT])
        for tt in range(4):
            for c2 in range(2):
                pg, pgr = pG.next()
                for kc in range(32):
                    mm(pg[:], uT[:, kc, tt * 128:(tt + 1) * 128], w2[:, kc, c2 * 512:(c2 + 1) * 512], kc == 0, kc == 31,
                       [r_uT, r_w2], [pgr])
                P.add("dve", lambda e, pg=pg, tt=tt, c2=c2: e.tensor_tensor(
                    out=x1_in[:, tt, c2 * 512:(c2 + 1) * 512], in0=pg[:], in1=x1_in[:, tt, c2 * 512:(c2 + 1) * 512], op=ALU.add),
                    reads=[pgr, r_x1], writes=[r_x1])
        for tt in range(4):
            P.add("act", lambda e, tt=tt: e.activation(out=junk4[:], in_=x1_in[:, tt, :], func=AF.Square,
                                                       accum_out=ss4[:, tt:tt + 1]), reads=[r_x1], writes=[r_j4, r_ss4])
        P.add("dve", lambda e: e.tensor_scalar(out=rs4[:], in0=ss4[:], scalar1=1.0 / D, scalar2=1e-6, op0=ALU.mult, op1=ALU.add),
              reads=[r_ss4], writes=[r_rs4])
        P.add("act", lambda e: e.activation(out=rs4[:], in_=rs4[:], func=AF.Ln), reads=[r_rs4], writes=[r_rs4])
        P.add("act", lambda e: e.activation(out=rs4[:], in_=rs4[:], func=AF.Exp, scale=-0.5), reads=[r_rs4], writes=[r_rs4])
        for tt in range(4):
            P.add("dve", lambda e, tt=tt: e.scalar_tensor_tensor(out=x1_in[:, tt, :], in0=x1_in[:, tt, :], scalar=rs4[:, tt:tt + 1],
                                                                 in1=gfin_sb[:], op0=ALU.mult, op1=ALU.mult),
                  reads=[r_x1, r_rs4, r_c4], writes=[r_x1])
        ro = P.res()
        out_res.append(ro)
        P.add("sp", lambda e, ci=ci: e.dma_start(out=out[ci * 512:(ci + 1) * 512, :].rearrange("(t p) f -> p t f", p=128),
                                                 in_=x1_in[:]), reads=[r_x1], writes=[ro], dma=True, semkey="outst")

    s4_load_h(0)
    for ci in range(NCI):
        s4_chunk(ci)
    P.add("sp", None, reads=out_res)
    P.emit()
    return nc


def _consts():
    bf = ml_dtypes.bfloat16
    ident = np.eye(128, dtype=np.float32).astype(bf)
    sw = np.zeros((128, 128), np.float32)
    for m in range(128):
        k = m + 32 if (m % 64) < 32 else m - 32
        sw[k, m] = 1.0
    pos = np.arange(S, dtype=np.float32)
    inv_freq = (10000.0 ** (-np.arange(0, 64, 2, dtype=np.float32) / 64)).astype(np.float32)
    ang = pos[:, None] * inv_freq[None, :]
    ang = np.concatenate([ang, ang], axis=-1)
    cos = np.cos(ang).astype(np.float32).T
    sin = np.sin(ang).astype(np.float32).T
    sgn = np.where(np.arange(64) < 32, -1.0, 1.0).astype(np.float32)[:, None]
    cosT = np.ascontiguousarray(np.concatenate([cos, cos], 0))
    sinT = np.ascontiguousarray(np.concatenate([sin * sgn, sin * sgn], 0))
    k = np.arange(128)[:, None]
    c_ = np.arange(256)[None, :]
    band = (k <= c_) & (c_ <= k + 128)
    seam = band & ((k < 64) == (c_ < 128))
    mask2 = np.where(np.stack([band, seam], 1), 0.0, -30000.0).astype(np.float32)
    return dict(ident=ident, rswap=sw.astype(bf), cosT=cosT, sinT=sinT, mask2=mask2.astype(bf))


_NC_CACHE = {}


def kernel(x, w_in, w_branch_a, w_branch_b, w_out, lambda_q1, lambda_k1, lambda_q2, lambda_k2,
           diff_subln_g, norm_mix_g, norm_mlp_g, w_ff1, w_ff2, norm_final_g):
    f = lambda a: np.ascontiguousarray(np.asarray(a, dtype=np.float32))
    x = f(x)
    if "nc" not in _NC_CACHE:
        _NC_CACHE["nc"] = build()
    nc = _NC_CACHE["nc"]
    c = _consts()
    lamv = np.stack([f(lambda_q1)[0], f(lambda_k1)[0], f(lambda_q2)[0], f(lambda_k2)[0]], 0)
    shared = dict(
        w_in=f(w_in)[0], w_a=f(w_branch_a)[0], w_b=f(w_branch_b)[0], w_out=f(w_out)[0], w_ff1=f(w_ff1)[0], w_ff2=f(w_ff2)[0],
        lamv=np.ascontiguousarray(np.broadcast_to(lamv[None], (128, 4, 64))),
        subg=np.ascontiguousarray(np.broadcast_to(f(diff_subln_g)[0][None], (128, 128))),
        gmix=np.ascontiguousarray(f(norm_mix_g)[0].reshape(8, 128).T),
        gmlp=np.ascontiguousarray(f(norm_mlp_g)[0].reshape(8, 128).T),
        gfin=np.ascontiguousarray(np.broadcast_to(f(norm_final_g)[None], (128, D))),
        **c,
    )
    in_maps = []
    for i in range(NCORES):
        d = dict(shared)
        d["x"] = x[i * NB:(i + 1) * NB].reshape(NB * S, D)
        in_maps.append(d)
    res = run_bass_kernel_spmd(nc, in_maps, core_ids=list(range(NCORES)))
    outs = [np.asarray(r["out"]).reshape(NB, S, D) for r in res.results]
    return np.concatenate(outs, 0).astype(np.float32)
```

```python
import os
import numpy as np
import ml_dtypes
import concourse.bass as bass
import concourse.mybir as mybir
from concourse.bass_utils import run_bass_kernel_spmd

F32 = mybir.dt.float32
BF16 = mybir.dt.bfloat16
AF = mybir.ActivationFunctionType
ALU = mybir.AluOpType

NCORES = 8
S = 4096
D = 1024
NB = 2
DFF = 4096
NCH = S // 512


class Res:
    __slots__ = ("name", "writer", "readers", "excl")

    def __init__(self, name, excl=False):
        self.name = name
        self.excl = excl
        self.writer = None
        self.readers = {}


class Op:
    __slots__ = ("eng", "fn", "deps", "dma", "semkey", "signal", "event", "idx", "raw")


class Prog:
    ENGS = ("pe", "act", "dve", "pool", "sp")
    SEM_LIMIT = 30000

    def __init__(self, nc):
        self.nc = nc
        self.ops = []
        self.last = {}
        self.last_dma = {}
        self.fence_deps = {}

    def res(self, name=None):
        return Res(name)

    def add(self, eng, fn, reads=(), writes=(), dma=False, semkey=None):
        if fn is not None and len(self.ops) >= int(os.environ.get("MK_MAXOPS", "1000000000")):
            return None
        op = Op()
        op.eng = eng
        op.fn = fn
        op.dma = dma
        op.semkey = semkey
        op.signal = dma
        op.event = None
        op.idx = len(self.ops)
        deps = {}
        raw = set()
        writes = list(writes) + [r for r in reads if r.excl and r not in writes]
        for r in reads:
            if r.writer is not None:
                deps[id(r.writer)] = r.writer
                raw.add(id(r.writer))
        for w in writes:
            if w.writer is not None:
                deps[id(w.writer)] = w.writer
            for d in w.readers.values():
                deps[id(d)] = d
        fd = self.fence_deps.pop(eng, None)
        if fd:
            for d in fd:
                deps[id(d)] = d
        key = ("dma", semkey) if dma else eng
        for r in reads:
            r.readers[key] = op
        for w in writes:
            w.writer = op
            w.readers = {}
        op.deps = list(deps.values())
        op.raw = raw
        self.ops.append(op)
        if dma:
            self.last_dma[semkey] = op
        else:
            self.last[eng] = op
        return op

    def fence(self):
        alld = [o for o in self.last.values() if o.fn is not None] + list(self.last_dma.values())
        for e in self.ENGS:
            self.fence_deps[e] = list(alld)

    def emit(self):
        nc = self.nc
        ops = self.ops
        print("MK ops", len(ops), flush=True)
        for op in ops:
            for d in op.deps:
                if d.dma:
                    continue
                if d.eng == op.eng and not op.dma and (d.eng == "pe" or id(d) not in op.raw):
                    continue
                d.signal = True
        eng_sem, eng_cnt, dma_sem, dma_cnt = {}, {}, {}, {}
        for op in ops:
            if op.fn is None or not op.signal:
                continue
            if op.dma:
                k = op.semkey
                if k not in dma_sem or dma_cnt[k] >= self.SEM_LIMIT:
                    dma_sem[k] = nc.alloc_semaphore(name=f"d_{k}_{op.idx}")
                    dma_cnt[k] = 0
                dma_cnt[k] += 16
                op.event = (dma_sem[k], dma_cnt[k])
            else:
                e = op.eng
                if e not in eng_sem or eng_cnt[e] >= self.SEM_LIMIT:
                    eng_sem[e] = nc.alloc_semaphore(name=f"e_{e}_{op.idx}")
                    eng_cnt[e] = 0
                eng_cnt[e] += 1
                op.event = (eng_sem[e], eng_cnt[e])
        per_eng = {e: [] for e in self.ENGS}
        for op in ops:
            per_eng[op.eng].append(op)

        def run(engobj, lst):
            waited = {}
            for op in lst:
                for d in op.deps:
                    if d.event is None:
                        continue
                    if (not d.dma) and d.eng == op.eng and not op.dma and (d.eng == "pe" or id(d) not in op.raw):
                        continue
                    sem, val = d.event
                    k = id(sem)
                    if waited.get(k, 0) < val:
                        engobj.wait_ge(sem, val)
                        waited[k] = val
                if op.fn is None:
                    continue
                ins = op.fn(engobj)
                if op.signal:
                    ins.then_inc(op.event[0], 16 if op.dma else 1)

        with nc.Block() as block:
            @block.tensor
            def _(e):
                run(e, per_eng["pe"])

            @block.scalar
            def _(e):
                run(e, per_eng["act"])

            @block.vector
            def _(e):
                run(e, per_eng["dve"])

            @block.gpsimd
            def _(e):
                run(e, per_eng["pool"])

            @block.sync
            def _(e):
                run(e, per_eng["sp"])


class Alloc:
    def __init__(self, nc, prefix, base=16512):
        self.nc, self.prefix, self.off = nc, prefix, base

    def t(self, name, shape, dtype):
        n = 1
        for s in shape[1:]:
            n *= s
        nbytes = n * (2 if dtype == BF16 else 4)
        nbytes = (nbytes + 63) // 64 * 64
        h = self.nc.alloc_sbuf_tensor_at(f"{self.prefix}_{name}", list(shape), dtype, offset=self.off)
        self.off += nbytes
        assert self.off <= 229344, (self.prefix, name, self.off)
        return h


class Rot:
    def __init__(self, items):
        self.items = items
        self.i = 0

    def next(self):
        it = self.items[self.i % len(self.items)]
        self.i += 1
        return it


def build(debug=False, stop_after=None):
    nc = bass.Bass("TRN2", target_bir_lowering=False)

    def din(name, shape, dt=F32):
        return nc.dram_tensor(name, list(shape), dt, kind="ExternalInput").ap()

    def dscr(name, shape, dt):
        return nc.dram_tensor(name, list(shape), dt, kind="ExternalOutput" if debug else "Internal").ap()

    x = din("x", [NB * S, D])
    w_in = din("w_in", [D, 5120])
    w_a = din("w_a", [512, D])
    w_b = din("w_b", [512, D])
    w_out = din("w_out", [D, D])
    w_ff1 = din("w_ff1", [D, DFF])
    w_ff2 = din("w_ff2", [DFF, D])
    lamv = din("lamv", [128, 4, 64])
    subg = din("subg", [128, 128])
    gmix = din("gmix", [128, 8])
    gmlp = din("gmlp", [128, 8])
    gfin = din("gfin", [128, D])
    cosT = din("cosT", [128, S])
    sinT = din("sinT", [128, S])
    ident = din("ident", [128, 128], BF16)
    rswap = din("rswap", [128, 128], BF16)
    mask2 = din("mask2", [128, 2, 256], BF16)
    out = nc.dram_tensor("out", [NB * S, D], F32, kind="ExternalOutput").ap()

    QKT = dscr("QKT", [NB, 2048, S], BF16)
    VA = dscr("VA", [NB, 4, S, 130], BF16)
    VB = dscr("VB", [NB, 4, 128, 32, 129], BF16)
    GT = dscr("GT", [NB, 2048, S], BF16)
    OT = dscr("OT", [NB, 1024, S], BF16)
    X1 = dscr("X1", [NB * S, D], F32)
    H2T = dscr("H2T", [NB, 1024, S], BF16)

    P = Prog(nc)
    ps = [nc.alloc_psum_tensor(f"ps{i}", [128, 512], F32) for i in range(8)]
    psr = [Res(f"ps{i}", excl=True) for i in range(8)]
    out_res = []

    def mm(o, lhsT, rhs, start, stop, reads, writes):
        P.add("pe", lambda e: e.matmul(o, lhsT=lhsT, rhs=rhs, start=start, stop=stop), reads=reads, writes=writes)

    A = Alloc(nc, "s1")
    w_in_bf = A.t("w_in_bf", [128, 8, 5120], BF16)
    idb = A.t("idb", [128, 128], BF16)
    rsw = A.t("rsw", [128, 128], BF16)
    cs_sb = [A.t(f"cs{i}", [128, 2, 512], F32) for i in range(2)]
    gmix_sb = A.t("gmix", [128, 8], F32)
    xin = [A.t(f"xin{i}", [128, 4, D], F32) for i in range(2)]
    junk = A.t("junk", [128, D], BF16)
    ss = A.t("ss", [128, 4], F32)
    rstd = A.t("rstd", [128, 4], F32)
    hb = A.t("hb", [128, 4, D], BF16)
    hT = [A.t(f"hT{i}", [128, 8, 512], BF16) for i in range(2)]
    tb = [A.t(f"tb{i}", [128, 512], BF16) for i in range(2)]
    t1 = [A.t(f"t1{i}", [128, 512], F32) for i in range(2)]
    t2 = [A.t(f"t2{i}", [128, 512], F32) for i in range(2)]
    qk_st = [A.t(f"qkst{i}", [128, 4, 512], BF16) for i in range(2)]
    g_st = [A.t(f"gst{i}", [128, 4, 512], BF16) for i in range(2)]
    va_st = [A.t(f"vast{i}", [128, 4, 4, 130], BF16) for i in range(2)]
    vb_st = [A.t(f"vbst{i}", [128, 4, 4, 129], BF16) for i in range(2)]

    r_w = P.res("w_in_bf")
    r_const = P.res("const1")
    r_cs = [P.res() for _ in range(2)]
    r_xin = [P.res() for _ in range(2)]
    r_junk, r_ss, r_rstd, r_hb = P.res(), P.res(), P.res(), P.res()
    r_hT = [P.res() for _ in range(2)]
    tbR = Rot([(tb[i], P.res()) for i in range(2)])
    t1R = Rot([(t1[i], P.res()) for i in range(2)])
    t2R = Rot([(t2[i], P.res()) for i in range(2)])
    qkR = Rot([(qk_st[i], P.res()) for i in range(2)])
    gR = Rot([(g_st[i], P.res()) for i in range(2)])
    vaR = Rot([(va_st[i], P.res()) for i in range(2)])
    vbR = Rot([(vb_st[i], P.res()) for i in range(2)])
    pTR = Rot([(ps[0], psr[0]), (ps[1], psr[1])])
    prR = Rot([(ps[2], psr[2]), (ps[3], psr[3])])
    pjR = Rot([(ps[4], psr[4]), (ps[5], psr[5]), (ps[6], psr[6]), (ps[7], psr[7])])

    for dst, src, k in ((idb, ident, "c_id"), (rsw, rswap, "c_rs"), (gmix_sb, gmix, "c_gm")):
        P.add("sp", lambda e, dst=dst, src=src: e.dma_start(out=dst[:], in_=src), writes=[r_const], dma=True, semkey=k)
    for i in range(2):
        P.add("pool", lambda e, i=i: e.memset(va_st[i][:], 1.0), writes=[vaR.items[i][1]])
        P.add("pool", lambda e, i=i: e.memset(vb_st[i][:], 1.0), writes=[vbR.items[i][1]])
    r_wk = [P.res() for _ in range(8)]
    for kc in range(8):
        P.add("pool", lambda e, kc=kc: e.dma_start(out=w_in_bf[:, kc, :], in_=w_in[kc * 128:(kc + 1) * 128, :]),
              writes=[r_wk[kc]], dma=True, semkey=f"w_in{kc}")
        if kc % 2:
            P.add("dve", lambda e, kc=kc: e.tensor_scalar_mul(out=w_in_bf[:, kc, :], in0=w_in_bf[:, kc, :],
                                                               scalar1=gmix_sb[:, kc:kc + 1]),
                  reads=[r_wk[kc], r_const], writes=[r_wk[kc], r_w])
        else:
            P.add("act", lambda e, kc=kc: e.activation(out=w_in_bf[:, kc, :], in_=w_in_bf[:, kc, :], func=AF.Copy,
                                                        scale=gmix_sb[:, kc:kc + 1]),
                  reads=[r_wk[kc], r_const], writes=[r_wk[kc], r_w])

    QK_COL0 = [0, 128, 256, 384, 512, 640, 768, 896, 1536, 1664, 1792, 1920, 2048, 2176, 2304, 2432]

    def s1_load(ci):
        sl = ci % 2
        P.add("sp", lambda e: e.dma_start(out=xin[sl][:], in_=x[ci * 512:(ci + 1) * 512, :].rearrange(
            "(t p) f -> p t f", p=128)), writes=[r_xin[sl]], dma=True, semkey=f"xin{sl}")
        cc = ci % NCH
        P.add("sp", lambda e: e.dma_start(out=cs_sb[sl][:, 0, :], in_=cosT[:, cc * 512:(cc + 1) * 512]),
              writes=[r_cs[sl]], dma=True, semkey=f"cs{sl}")
        P.add("sp", lambda e: e.dma_start(out=cs_sb[sl][:, 1, :], in_=sinT[:, cc * 512:(cc + 1) * 512]),
              writes=[r_cs[sl]], dma=True, semkey=f"cs{sl}")

    def s1_norm(ci):
        sl = ci % 2
        for tt in range(4):
            P.add("act", lambda e, tt=tt: e.activation(out=junk[:], in_=xin[sl][:, tt, :], func=AF.Square,
                                                       accum_out=ss[:, tt:tt + 1]),
                  reads=[r_xin[sl]], writes=[r_junk, r_ss])
        P.add("dve", lambda e: e.tensor_scalar(out=rstd[:], in0=ss[:], scalar1=1.0 / D, scalar2=1e-6,
                                               op0=ALU.mult, op1=ALU.add), reads=[r_ss], writes=[r_rstd])
        P.add("act", lambda e: e.activation(out=rstd[:], in_=rstd[:], func=AF.Ln), reads=[r_rstd], writes=[r_rstd])
        P.add("act", lambda e: e.activation(out=rstd[:], in_=rstd[:], func=AF.Exp, scale=-0.5),
              reads=[r_rstd], writes=[r_rstd])
        for tt in range(4):
            P.add("act", lambda e, tt=tt: e.activation(out=hb[:, tt, :], in_=xin[sl][:, tt, :], func=AF.Copy,
                                                       scale=rstd[:, tt:tt + 1]),
                  reads=[r_xin[sl], r_rstd], writes=[r_hb])
        for fc in range(8):
            pt, ptr = pTR.next()
            ptb = pt.bitcast(BF16)
            for tt in range(4):
                P.add("pe", lambda e, tt=tt, fc=fc, ptb=ptb: e.transpose(
                    out=ptb[:, tt * 128:(tt + 1) * 128], in_=hb[:, tt, fc * 128:(fc + 1) * 128], identity=idb[:]),
                    reads=[r_hb, r_const], writes=[ptr])
            P.add("dve", lambda e, fc=fc, ptb=ptb: e.tensor_copy(out=hT[sl][:, fc, :], in_=ptb[:, 0:512]),
                  reads=[ptr], writes=[r_hT[sl]])

    def s1_qk(ci):
        sl = ci % 2
        b, cc = divmod(ci, NCH)
        pend = []

        def finish(pq, pqr, tbb, tbr, st, str_, m, grp, last):
            pr, prr = prR.next()
            mm(pr[:], rsw[:], tbb[:], True, True, [tbr, r_const], [prr])
            a1, a1r = t1R.next()
            a2, a2r = t2R.next()
            P.add("dve", lambda e: e.tensor_tensor(out=a1[:], in0=pq[:], in1=cs_sb[sl][:, 0, :], op=ALU.mult),
                  reads=[pqr, r_cs[sl]], writes=[a1r])
            P.add("dve", lambda e: e.tensor_tensor(out=a2[:], in0=pr[:], in1=cs_sb[sl][:, 1, :], op=ALU.mult),
                  reads=[prr, r_cs[sl]], writes=[a2r])
            P.add("pool", lambda e: e.tensor_tensor(out=st[:, m, :], in0=a1[:], in1=a2[:], op=ALU.add),
                  reads=[a1r, a2r], writes=[str_])
            if last:
                P.add("sp", lambda e: e.dma_start(
                    out=QKT[b, grp * 512:(grp + 1) * 512, cc * 512:(cc + 1) * 512].rearrange("(m p) s -> p m s", p=128),
                    in_=st[:]), reads=[str_], writes=[], dma=True, semkey=f"qkst{grp % 2}")

        for grp in range(4):
            st, str_ = qkR.next()
            for m in range(4):
                mc = grp * 4 + m
                c0 = QK_COL0[mc]
                pq, pqr = pjR.next()
                for kc in range(8):
                    mm(pq[:], w_in_bf[:, kc, c0:c0 + 128], hT[sl][:, kc, :], kc == 0, kc == 7,
                       [r_w, r_hT[sl]], [pqr])
                tbb, tbr = tbR.next()
                P.add("act", lambda e, tbb=tbb, pq=pq: e.activation(out=tbb[:], in_=pq[:], func=AF.Copy),
                      reads=[pqr], writes=[tbr])
                if pend:
                    finish(*pend.pop())
                pend.append((pq, pqr, tbb, tbr, st, str_, m, grp, m == 3))
        return [(lambda t=t: finish(*t)) for t in pend]

    def s1_v(ci, pend_qk=None):
        sl = ci % 2
        b, cc = divmod(ci, NCH)
        va, var_ = vaR.next()
        vb, vbr = vbR.next()
        for tt in range(4):
            for which in range(2):
                c0 = 1024 if which == 0 else 2560
                pv, pvr = pjR.next()
                for kc in range(8):
                    mm(pv[:], hT[sl][:, kc, tt * 128:(tt + 1) * 128], w_in_bf[:, kc, c0:c0 + 512], kc == 0, kc == 7,
                       [r_w, r_hT[sl]], [pvr])
                if pend_qk:
                    pend_qk.pop()()
                if which == 0:
                    for hh in range(2):
                        P.add("dve", lambda e, pv=pv, tt=tt, hh=hh: e.tensor_copy(
                            out=va[:, :, tt, 65 * hh:65 * hh + 64],
                            in_=pv[:].rearrange("p (r h d) -> p r h d", h=2, d=64)[:, :, hh, :]),
                            reads=[pvr], writes=[var_])
                else:
                    P.add("act", lambda e, pv=pv, tt=tt: e.activation(
                        out=vb[:, :, tt, 0:128], in_=pv[:].rearrange("p (h d) -> p h d", d=128), func=AF.Copy),
                        reads=[pvr], writes=[vbr])
        for pr_ in range(4):
            P.add("sp", lambda e, pr_=pr_: e.dma_start(
                out=VA[b, pr_, cc * 512:(cc + 1) * 512, :].rearrange("(t p) c -> p t c", p=128),
                in_=va[:, pr_, :, :]), reads=[var_], writes=[], dma=True,
                semkey=f"vast{vaR.i % 2}")
        P.add("sp", lambda e: e.dma_start(
            out=VB[b, :, :, cc * 4:(cc + 1) * 4, :].rearrange("h p t c -> p h (t c)"),
            in_=vb[:].rearrange("p h t c -> p h (t c)")), reads=[vbr], writes=[], dma=True,
            semkey=f"vbst{vbR.i % 2}")

    def s1_g(ci):
        sl = ci % 2
        b, cc = divmod(ci, NCH)
        for grp in range(4):
            st, str_ = gR.next()
            for m in range(4):
                c0 = 3072 + (grp * 4 + m) * 128
                pg, pgr = pjR.next()
                for kc in range(8):
                    mm(pg[:], w_in_bf[:, kc, c0:c0 + 128], hT[sl][:, kc, :], kc == 0, kc == 7,
                       [r_w, r_hT[sl]], [pgr])
                P.add("act", lambda e, pg=pg, st=st, m=m: e.activation(out=st[:, m, :], in_=pg[:], func=AF.Sigmoid),
                      reads=[pgr], writes=[str_])
            P.add("sp", lambda e, st=st, grp=grp: e.dma_start(
                out=GT[b, grp * 512:(grp + 1) * 512, cc * 512:(cc + 1) * 512].rearrange("(m p) s -> p m s", p=128),
                in_=st[:]), reads=[str_], writes=[], dma=True, semkey=f"gst{gR.i % 2}")

    r_scr = P.res("scratch")
    NCI = NB * NCH
    if stop_after == 0:
        P.fence()
        P.add("sp", None)
        P.emit()
        return nc
    if stop_after is not None and 0 < stop_after < 1:
        NCI = 1
    s1_load(0)
    s1_norm(0)
    for ci in range(NCI):
        if stop_after == 0.1:
            break
        if ci + 1 < NCI:
            s1_load(ci + 1)
        pend_qk = s1_qk(ci)
        if stop_after == 0.2:
            break
        if ci + 1 < NCI:
            s1_norm(ci + 1)
        s1_v(ci, pend_qk)
        if stop_after == 0.3:
            break
        s1_g(ci)
    P.fence()
    if stop_after is not None and stop_after <= 1:
        P.add("sp", None)
        P.emit()
        return nc

    A = Alloc(nc, "s2")
    idb2 = A.t("idb", [128, 128], BF16)
    lam_sb = A.t("lamv", [128, 4, 64], F32)
    lam_t = A.t("lamt", [128, 8], F32)
    subg_sb = A.t("subg", [128, 128], F32)
    kt_sb = [A.t(f"kt{i}", [128, S], BF16) for i in range(2)]
    qt_sb = [A.t(f"qt{i}", [128, S], BF16) for i in range(2)]
    v_sb = [A.t(f"v{i}", [128, 32, 130], BF16) for i in range(2)]
    NE = 6
    e_sb = [A.t(f"e{i}", [128, 512], BF16) for i in range(NE)]
    o_sb = A.t("o", [128, 128], F32)
    o2_sb = A.t("o2", [128, 128], F32)
    ojunk = A.t("ojunk", [128, 128], F32)
    sm = A.t("sm", [128, 8], F32)
    ob = [A.t(f"ob{i}", [128, 128], BF16) for i in range(8)]
    acc_sb = [A.t(f"accsb{i}", [128, 8, 129], F32) for i in range(2)]
    ot_st = [A.t(f"otst{i}", [128, 512], BF16) for i in range(2)]

    r_c2 = P.res("const2")
    r_kt = [P.res() for _ in range(2)]
    r_qt = [P.res() for _ in range(2)]
    r_v = [P.res() for _ in range(2)]
    eR = Rot([(e_sb[i], P.res()) for i in range(NE)])
    r_o, r_o2, r_oj, r_sm = P.res(), P.res(), P.res(), P.res()
    obR = Rot([(ob[i], P.res()) for i in range(8)])
    accR = Rot([(acc_sb[i], P.res()) for i in range(2)])
    otR = Rot([(ot_st[i], P.res()) for i in range(2)])
    psS = Rot([(ps[i], psr[i]) for i in range(4)])
    for dst, src, k in ((idb2, ident, "c2_id"), (lam_sb, lamv, "c2_lam"), (subg_sb, subg, "c2_sg")):
        P.add("sp", lambda e, dst=dst, src=src: e.dma_start(out=dst[:], in_=src), writes=[r_c2], dma=True, semkey=k)
    P.add("dve", lambda e: e.tensor_tensor(out=lam_sb[:, 0, :], in0=lam_sb[:, 0, :], in1=lam_sb[:, 1, :], op=ALU.mult),
          reads=[r_c2], writes=[r_c2])
    P.add("dve", lambda e: e.tensor_tensor(out=lam_sb[:, 2, :], in0=lam_sb[:, 2, :], in1=lam_sb[:, 3, :], op=ALU.mult),
          reads=[r_c2], writes=[r_c2])
    P.add("dve", lambda e: e.tensor_reduce(out=lam_t[:, 0:1], in_=lam_sb[:, 0, :], axis=mybir.AxisListType.X, op=ALU.add),
          reads=[r_c2], writes=[r_c2])
    P.add("dve", lambda e: e.tensor_reduce(out=lam_t[:, 1:2], in_=lam_sb[:, 2, :], axis=mybir.AxisListType.X, op=ALU.add),
          reads=[r_c2], writes=[r_c2])
    P.add("act", lambda e: e.activation(out=lam_t[:, 0:2], in_=lam_t[:, 0:2], func=AF.Exp), reads=[r_c2], writes=[r_c2])
    P.add("dve", lambda e: e.scalar_tensor_tensor(out=lam_t[:, 2:3], in0=lam_t[:, 1:2], scalar=-0.2, in1=lam_t[:, 0:1],
                                                  op0=ALU.add, op1=ALU.subtract), reads=[r_c2], writes=[r_c2])
    P.add("dve", lambda e: e.tensor_scalar(out=subg_sb[:], in0=subg_sb[:], scalar1=0.8, scalar2=None, op0=ALU.mult),
          reads=[r_c2], writes=[r_c2])

    def acc_ap(i, width):
        bank = 4 + i // 3
        slot = i % 3
        return ps[bank][:, slot * 129: slot * 129 + width], psr[bank]

    def attn_pass(b, kt, qt, vt, r_in, k_rows, q_rows, nmaps, vcol0, vw, j, masked):
        if masked:
            kts = list(range(max(0, 4 * j - 8), min(31, 4 * j + 11) + 1))
        else:
            kts = list(range(32))
        accs = [[acc_ap(c * 4 + q4, vw) for q4 in range(4)] for c in range(nmaps)]
        pend = []

        def issue_qk(kti):
            lst = []
            for c in range(nmaps):
                pS, pSr = psS.next()
                mm(pS[:], kt[k_rows[c]:k_rows[c] + 64, kti * 128:(kti + 1) * 128],
                   qt[q_rows[c]:q_rows[c] + 64, j * 512:(j + 1) * 512], True, True, r_in, [pSr])
                eb, ebr = eR.next()
                P.add("act", lambda e, eb=eb, pS=pS: e.activation(out=eb[:], in_=pS[:], func=AF.Exp, scale=0.125),
                      reads=[pSr], writes=[ebr])
                if masked:
                    mi = kti - 4 * j + 8
                    P.add("dve", lambda e, eb=eb, mi=mi: e.tensor_tensor(out=eb[:], in0=eb[:], in1=mask_sb[:, mi, :],
                                                                         op=ALU.mult), reads=[ebr, r_c2], writes=[ebr])
                lst.append((eb, ebr))
            return lst

        started = set()

        def issue_pv(kti, lst, first, last):
            for c in range(nmaps):
                eb, ebr = lst[c]
                for q4 in range(4):
                    ap_, rr = accs[c][q4]
                    st_ = first and id(rr) not in started
                    started.add(id(rr))
                    mm(ap_, eb[:, q4 * 128:(q4 + 1) * 128], vt[:, kti, vcol0:vcol0 + vw], st_, last,
                       [ebr] + r_in, [rr])

        def issue_qk1(kti, c):
            pS, pSr = psS.next()
            mm(pS[:], kt[k_rows[c]:k_rows[c] + 64, kti * 128:(kti + 1) * 128],
               qt[q_rows[c]:q_rows[c] + 64, j * 512:(j + 1) * 512], True, True, r_in, [pSr])
            eb, ebr = eR.next()
            P.add("act", lambda e, eb=eb, pS=pS: e.activation(out=eb[:], in_=pS[:], func=AF.Exp, scale=0.125),
                  reads=[pSr], writes=[ebr])
            return eb, ebr

        def issue_pv1(kti, c, eb, ebr, first, last):
            for q4 in range(4):
                ap_, rr = accs[c][q4]
                st_ = first and id(rr) not in started
                started.add(id(rr))
                mm(ap_, eb[:, q4 * 128:(q4 + 1) * 128], vt[:, kti, vcol0:vcol0 + vw], st_, last, [ebr] + r_in, [rr])

        prev = None
        for n_, kti in enumerate(kts):
            cur = [issue_qk1(kti, c) for c in range(nmaps)]
            if prev is not None:
                for c in range(nmaps):
                    issue_pv1(prev[0], c, prev[1][c][0], prev[1][c][1], prev[0] == kts[0], False)
            prev = (kti, cur)
        for c in range(nmaps):
            issue_pv1(prev[0], c, prev[1][c][0], prev[1][c][1], prev[0] == kts[0], True)
        asb, asr = accR.next()
        nb_ = (nmaps * 4 + 2) // 3
        for k_ in range(nb_):
            w_ = min(3, nmaps * 4 - 3 * k_)
            P.add("dve", lambda e, k_=k_, w_=w_: e.tensor_copy(
                out=asb[:, 3 * k_:3 * k_ + w_, :].rearrange("p a b -> p (a b)"), in_=ps[4 + k_][:, 0:129 * w_]),
                reads=[psr[4 + k_]], writes=[asr])
        return [[(asb[:, c * 4 + q4, :], asr) for q4 in range(4)] for c in range(nmaps)]

    def load_b(b, h, sl):
        P.add("sp", lambda e: e.dma_start(out=kt_sb[sl][:], in_=QKT[b, 1536 + 128 * h:1536 + 128 * (h + 1), :]),
              writes=[r_kt[sl]], dma=True, semkey=f"kt{sl}")
        P.add("sp", lambda e: e.dma_start(out=qt_sb[sl][:], in_=QKT[b, 1024 + 128 * h:1024 + 128 * (h + 1), :]),
              writes=[r_qt[sl]], dma=True, semkey=f"qt{sl}")
        P.add("sp", lambda e: e.dma_start(
            out=v_sb[sl][:, :, 0:129], in_=VB[b, h]),
            writes=[r_v[sl]], dma=True, semkey=f"v{sl}")

    bh = [(b, h) for b in range(NB) for h in range(4)]
    load_b(0, 0, 0)
    deferred_b = []
    for n, (b, h) in enumerate(bh):
        sl = n % 2
        if n + 1 < len(bh):
            load_b(bh[n + 1][0], bh[n + 1][1], (n + 1) % 2)
        r_in = [r_kt[sl], r_qt[sl], r_v[sl]]
        for j in range(NCH):
            accs = attn_pass(b, kt_sb[sl], qt_sb[sl], v_sb[sl], r_in, [0, 64], [0, 64], 2, 0, 129, j, False)
            while deferred_b:
                deferred_b.pop(0)()
            st, str_ = otR.next()
            tr_list = []
            for q4 in range(4):
                a1, a1r = accs[0][q4]
                a2, a2r = accs[1][q4]
                P.add("dve", lambda e, a1=a1: e.reciprocal(out=sm[:, 0:1], in_=a1[:, 128:129]), reads=[a1r], writes=[r_sm])
                P.add("dve", lambda e, a2=a2: e.reciprocal(out=sm[:, 1:2], in_=a2[:, 128:129]), reads=[a2r], writes=[r_sm])
                P.add("dve", lambda e: e.tensor_tensor(out=sm[:, 2:3], in0=sm[:, 1:2], in1=lam_t[:, 2:3], op=ALU.mult),
                      reads=[r_sm, r_c2], writes=[r_sm])
                P.add("dve", lambda e, a1=a1: e.tensor_scalar(out=o_sb[:], in0=a1[:, 0:128], scalar1=sm[:, 0:1],
                                                              scalar2=None, op0=ALU.mult), reads=[a1r, r_sm], writes=[r_o])
                P.add("dve", lambda e, a2=a2: e.scalar_tensor_tensor(out=o2_sb[:], in0=a2[:, 0:128], scalar=sm[:, 2:3],
                                                                    in1=o_sb[:], op0=ALU.mult, op1=ALU.add),
                      reads=[a2r, r_sm, r_o], writes=[r_o2])
                P.add("dve", lambda e: e.tensor_tensor(out=ojunk[:], in0=o2_sb[:], in1=o2_sb[:], op=ALU.mult),
                      reads=[r_o2], writes=[r_oj])
                P.add("dve", lambda e: e.tensor_reduce(out=sm[:, 3:4], in_=ojunk[:], axis=mybir.AxisListType.X, op=ALU.add),
                      reads=[r_oj], writes=[r_sm])
                P.add("dve", lambda e: e.tensor_scalar(out=sm[:, 3:4], in0=sm[:, 3:4], scalar1=1.0 / 128, scalar2=1e-5,
                                                       op0=ALU.mult, op1=ALU.add), reads=[r_sm], writes=[r_sm])
                P.add("act", lambda e: e.activation(out=sm[:, 4:5], in_=sm[:, 3:4], func=AF.Ln), reads=[r_sm], writes=[r_sm])
                P.add("act", lambda e: e.activation(out=sm[:, 4:5], in_=sm[:, 4:5], func=AF.Exp, scale=-0.5),
                      reads=[r_sm], writes=[r_sm])
                obb, obr = obR.next()
                P.add("dve", lambda e, obb=obb: e.scalar_tensor_tensor(out=obb[:], in0=o2_sb[:], scalar=sm[:, 4:5],
                                                                       in1=subg_sb[:], op0=ALU.mult, op1=ALU.mult),
                      reads=[r_o2, r_sm, r_c2], writes=[obr])
                tr_list.append((obb, obr, q4))

            def fin_b(tr_list=tr_list, st=st, str_=str_, b=b, h=h, j=j, key=f"otst{otR.i % 2}"):
                ptb = ps[7].bitcast(BF16)
                for (obb, obr, q4) in tr_list:
                    P.add("pe", lambda e, obb=obb, q4=q4: e.transpose(out=ptb[:, q4 * 128:(q4 + 1) * 128], in_=obb[:],
                                                                       identity=idb2[:]), reads=[obr, r_c2], writes=[psr[7]])
                P.add("dve", lambda e: e.tensor_copy(out=st[:], in_=ptb[:, 0:512]), reads=[psr[7]], writes=[str_])
                P.add("sp", lambda e: e.dma_start(
                    out=OT[b, 512 + 128 * h:512 + 128 * (h + 1), j * 512:(j + 1) * 512], in_=st[:]),
                    reads=[str_], writes=[], dma=True, semkey=key)
            deferred_b.append(fin_b)

    while deferred_b:
        deferred_b.pop(0)()
    P.fence()
    if stop_after == 1.5:
        P.add("sp", None)
        P.emit()
        return nc

    A = Alloc(nc, "s2a")
    PATS = (1, 4, 16)
    m2_sb = A.t("m2", [128, 2, 256], BF16)
    ones_f = A.t("onesf", [128, 64], F32)
    ida = A.t("ida", [128, 128], BF16)
    ktp = [[A.t(f"ktp{i}_{g}", [128, S + 128], BF16) for g in PATS] for i in range(2)]
    qtp = [[A.t(f"qtp{i}_{g}", [128, S], BF16) for g in PATS] for i in range(2)]
    vh = [[A.t(f"vh{i}_{g}", [128, 33, 130], BF16) for g in PATS] for i in range(2)]
    acc = [A.t(f"acc{i}", [128, S], F32) for i in range(2)]
    NE2 = 8
    e2 = [A.t(f"e{i}", [128, 256], BF16) for i in range(NE2)]
    oa_st = [A.t(f"oast{i}", [128, 512], BF16) for i in range(2)]
    rz_sb = [A.t(f"rz{i}", [128, 512], F32) for i in range(2)]
    r_c2a = P.res()
    r_ktp = [[P.res() for _ in PATS] for _ in range(2)]
    r_qtp = [[P.res() for _ in PATS] for _ in range(2)]
    r_vh = [[P.res() for _ in PATS] for _ in range(2)]
    r_acc = [P.res() for _ in range(2)]
    e2R = Rot([(e2[i], P.res()) for i in range(NE2)])
    oaR = Rot([(oa_st[i], P.res()) for i in range(2)])
    rzR = Rot([(rz_sb[i], P.res()) for i in range(2)])
    psS2 = Rot([(ps[i], psr[i]) for i in range(6)])
    psO2 = Rot([(ps[6 + i], psr[6 + i]) for i in range(2)])
    P.add("sp", lambda e: e.dma_start(out=m2_sb[:], in_=mask2), writes=[r_c2a], dma=True, semkey="c2a_m")
    P.add("sp", lambda e: e.dma_start(out=ida[:], in_=ident), writes=[r_c2a], dma=True, semkey="c2a_i")
    P.add("pool", lambda e: e.memset(ones_f[:], 1.0), writes=[r_c2a])
    for i in range(2):
        for gi in range(3):
            P.add("pool", lambda e, i=i, gi=gi: e.memset(ktp[i][gi][:, 0:64], 0.0), writes=[r_ktp[i][gi]])
            P.add("pool", lambda e, i=i, gi=gi: e.memset(ktp[i][gi][:, S + 64:S + 128], 0.0), writes=[r_ktp[i][gi]])
            P.add("pool", lambda e, i=i, gi=gi: e.memset(vh[i][gi][0:64, 0, :], 0.0), writes=[r_vh[i][gi]])
            P.add("pool", lambda e, i=i, gi=gi: e.memset(vh[i][gi][64:128, 32, :], 0.0), writes=[r_vh[i][gi]])

    def a_load(b, pr, sl):
        P.add("sp", lambda e: e.dma_start(out=ktp[sl][0][:, 64:64 + S], in_=QKT[b, 512 + 128 * pr:512 + 128 * (pr + 1), :]),
              writes=[r_ktp[sl][0]], dma=True, semkey=f"a_kt{sl}")
        P.add("sp", lambda e: e.dma_start(out=qtp[sl][0][:], in_=QKT[b, 128 * pr:128 * (pr + 1), :]),
              writes=[r_qtp[sl][0]], dma=True, semkey=f"a_qt{sl}")
        for gi, g in enumerate(PATS):
            n_ = S // g // 128
            if gi > 0:
                P.add("pool", lambda e, gi=gi, g=g: e.tensor_copy(
                    out=ktp[sl][gi][:, 64:64 + S].rearrange("p (r i) -> p r i", r=g),
                    in_=ktp[sl][0][:, 64:64 + S].rearrange("p (i r) -> p r i", r=g)),
                    reads=[r_ktp[sl][0]], writes=[r_ktp[sl][gi]])
                P.add("pool", lambda e, gi=gi, g=g: e.tensor_copy(
                    out=qtp[sl][gi][:].rearrange("p (r i) -> p r i", r=g),
                    in_=qtp[sl][0][:].rearrange("p (i r) -> p r i", r=g)),
                    reads=[r_qtp[sl][0]], writes=[r_qtp[sl][gi]])
            src = VA[b, pr].rearrange("(m j r) c -> r j m c", j=128, r=g)
            for sg in range(g):
                for m0 in range(0, n_, 8):
                    m1 = min(n_, m0 + 8)
                    P.add("sp", lambda e, gi=gi, sg=sg, n_=n_, src=src, m0=m0, m1=m1: e.dma_start(
                        out=vh[sl][gi][64:128, sg * n_ + m0:sg * n_ + m1, :], in_=src[sg, 0:64, m0:m1, :]),
                        writes=[r_vh[sl][gi]], dma=True, semkey=f"a_v{sl}_{gi}")
                    P.add("sp", lambda e, gi=gi, sg=sg, n_=n_, src=src, m0=m0, m1=m1: e.dma_start(
                        out=vh[sl][gi][0:64, sg * n_ + 1 + m0:sg * n_ + 1 + m1, :], in_=src[sg, 64:128, m0:m1, :]),
                        writes=[r_vh[sl][gi]], dma=True, semkey=f"a_v{sl}_{gi}")

    def a_head(b, pr, sl, hh, ai):
        ac, acr = acc[ai], r_acc[ai]
        for gi, g in enumerate(PATS):
            L = S // g
            n_ = L // 128
            tiles = []
            for u in range(33):
                if u == 0:
                    tiles.append((u, 0, 128, 1, 128))
                elif u == 32:
                    tiles.append((u, 128 * 31, 128, 1, 0))
                elif u % n_ == 0:
                    tiles.append((u, 128 * (u - 1), 256, 1, 0))
                else:
                    tiles.append((u, 128 * (u - 1), 256, 0, 0))
            banks = {}

            def qk(tl):
                u, q0, N, mk, mc0 = tl
                pS, pSr = psS2.next()
                mm(pS[:, 0:N], ktp[sl][gi][64 * hh:64 * hh + 64, 128 * u:128 * u + 128],
                   qtp[sl][gi][64 * hh:64 * hh + 64, q0:q0 + N], True, False, [r_ktp[sl][gi], r_qtp[sl][gi]], [pSr])
                mm(pS[:, 0:N], ida[:], m2_sb[:, mk, mc0:mc0 + N], False, True, [r_c2a], [pSr])
                eb, ebr = e2R.next()
                P.add("act", lambda e: e.activation(out=eb[:, 0:N], in_=pS[:, 0:N], func=AF.Exp, scale=0.125),
                      reads=[pSr], writes=[ebr])
                return eb, ebr

            def evac(w, po, por):
                if g == 1:
                    P.add("act", lambda e: e.activation(out=ac[0:65, 512 * w:512 * w + 512], in_=po[0:65, :], func=AF.Copy),
                          reads=[por], writes=[acr])
                    return
                pieces = [(512 * w // L, 512 * w % L, 0, 512)] if L >= 512 else \
                    [(2 * w, 0, 0, 256), (2 * w + 1, 0, 256, 256)]
                for (sg, i0_, c0, cnt) in pieces:
                    P.add("dve", lambda e, sg=sg, i0_=i0_, c0=c0, cnt=cnt, g=g: e.tensor_tensor(
                        out=ac[0:65, :].rearrange("p (i r) -> p r i", r=g)[:, sg, i0_:i0_ + cnt],
                        in0=po[0:65, c0:c0 + cnt],
                        in1=ac[0:65, :].rearrange("p (i r) -> p r i", r=g)[:, sg, i0_:i0_ + cnt], op=ALU.add),
                        reads=[por, acr], writes=[acr])

            def pv(tl, eb, ebr):
                u, q0, N, mk, mc0 = tl
                halves = []
                if u >= 1:
                    halves.append((u - 1, 0))
                if u <= 31:
                    halves.append((u, 0 if u == 0 else 128))
                for (v, c0) in halves:
                    w = v // 4
                    if w not in banks:
                        banks[w] = psO2.next()
                    po, por = banks[w]
                    opening = (v == u)
                    mm(po[0:65, (v % 4) * 128:(v % 4) * 128 + 128], vh[sl][gi][:, u, 65 * hh:65 * hh + 65],
                       eb[:, c0:c0 + 128], opening and v % 4 == 0, not opening, [ebr, r_vh[sl][gi]], [por])
                    if (not opening) and v % 4 == 3:
                        evac(w, po, por)

            queue = []
            for tl in tiles:
                queue.append((tl, qk(tl)))
                if len(queue) > 5:
                    t_, c_ = queue.pop(0)
                    pv(t_, *c_)
            for t_, c_ in queue:
                pv(t_, *c_)
        for j in range(NCH):
            pz, pzr = psO2.next()
            mm(pz[0:64, :], ones_f[64:65, 0:64], ac[64:65, j * 512:(j + 1) * 512], True, True, [acr, r_c2a], [pzr])
            rzb, rzr = rzR.next()
            P.add("dve", lambda e, pz=pz, rzb=rzb: e.reciprocal(out=rzb[0:64, :], in_=pz[0:64, :]), reads=[pzr], writes=[rzr])
            st, str_ = oaR.next()
            P.add("dve", lambda e, rzb=rzb, st=st, j=j: e.tensor_tensor(out=st[0:64, :], in0=rzb[0:64, :],
                                                                       in1=ac[0:64, j * 512:(j + 1) * 512], op=ALU.mult),
                  reads=[rzr, acr], writes=[str_])
            P.add("sp", lambda e, st=st, j=j: e.dma_start(
                out=OT[b, 128 * pr + 64 * hh:128 * pr + 64 * hh + 64, j * 512:(j + 1) * 512], in_=st[0:64, :]),
                reads=[str_], writes=[], dma=True, semkey=f"oast{oaR.i % 2}")

    bp = [(b, pr) for b in range(NB) for pr in range(4)]
    a_load(0, 0, 0)
    for n, (b, pr) in enumerate(bp):
        if n + 1 < len(bp):
            a_load(bp[n + 1][0], bp[n + 1][1], (n + 1) % 2)
        for hh in range(2):
            a_head(b, pr, n % 2, hh, hh)
    P.fence()
    if stop_after == 2:
        P.add("sp", None)
        P.emit()
        return nc

    A = Alloc(nc, "s3")
    idb3 = A.t("idb", [128, 128], BF16)
    wab = A.t("wab", [128, 8, D], BF16)
    wo = A.t("wo", [128, 8, D], BF16)
    ot_in = [A.t(f"otin{i}", [128, 8, 512], BF16) for i in range(2)]
    gt_in = [A.t(f"gtin{i}", [128, 16, 512], BF16) for i in range(2)]
    x_in = [A.t(f"xin{i}", [128, 4, D], F32) for i in range(2)]
    u1 = [A.t(f"u1{i}", [128, 512], F32) for i in range(2)]
    u2 = [A.t(f"u2{i}", [128, 512], F32) for i in range(2)]
    mT = A.t("mT", [128, 8, 512], BF16)
    junk3 = A.t("junk", [128, D], BF16)
    ss3 = A.t("ss", [128, 4], F32)
    rs3 = A.t("rs", [128, 4], F32)
    h2s = [A.t(f"h2_{i}", [128, 4, D], BF16) for i in range(2)]
    h2t_st = [A.t(f"h2t{i}", [128, 8, 512], BF16) for i in range(2)]
    r_c3 = P.res("const3")
    r_otin = [P.res() for _ in range(2)]
    r_gtin = [P.res() for _ in range(2)]
    r_x3 = [P.res() for _ in range(2)]
    u1R = Rot([(u1[i], P.res()) for i in range(2)])
    u2R = Rot([(u2[i], P.res()) for i in range(2)])
    r_mT, r_j3, r_ss3, r_rs3 = P.res(), P.res(), P.res(), P.res()
    r_h2s = [P.res(), P.res()]
    h2R = Rot([(h2t_st[i], P.res()) for i in range(2)])
    pA = Rot([(ps[0], psr[0]), (ps[1], psr[1])])
    pB = Rot([(ps[2], psr[2]), (ps[3], psr[3])])
    pO = Rot([(ps[4], psr[4]), (ps[5], psr[5])])
    pT3 = Rot([(ps[6], psr[6]), (ps[7], psr[7])])
    P.add("sp", lambda e: e.dma_start(out=idb3[:], in_=ident), writes=[r_c3], dma=True, semkey="c3_id")
    P.add("pool", lambda e: e.dma_start(out=wab[:, 0:4, :], in_=w_a.rearrange("(c p) n -> p c n", p=128)),
          writes=[r_c3], dma=True, semkey="c3_wa")
    P.add("pool", lambda e: e.dma_start(out=wab[:, 4:8, :], in_=w_b.rearrange("(c p) n -> p c n", p=128)),
          writes=[r_c3], dma=True, semkey="c3_wb")
    P.add("pool", lambda e: e.dma_start(out=wo[:], in_=w_out.rearrange("(c p) n -> p c n", p=128)),
          writes=[r_c3], dma=True, semkey="c3_wo")

    def s3_load(ci):
        sl = ci % 2
        b, cc = divmod(ci, NCH)
        P.add("sp", lambda e: e.dma_start(out=ot_in[sl][:], in_=OT[b, :, cc * 512:(cc + 1) * 512].rearrange(
            "(c p) s -> p c s", p=128)), writes=[r_otin[sl]], dma=True, semkey=f"otin{sl}")
        P.add("sp", lambda e: e.dma_start(out=gt_in[sl][:], in_=GT[b, :, cc * 512:(cc + 1) * 512].rearrange(
            "(c p) s -> p c s", p=128)), writes=[r_gtin[sl]], dma=True, semkey=f"gtin{sl}")
        P.add("sp", lambda e: e.dma_start(out=x_in[sl][:], in_=x[ci * 512:(ci + 1) * 512, :].rearrange(
            "(t p) f -> p t f", p=128)), writes=[r_x3[sl]], dma=True, semkey=f"x3{sl}")

    def s3_chunk(ci, prev_b=None):
        sl = ci % 2
        b, cc = divmod(ci, NCH)
        if ci + 1 < NCI:
            s3_load(ci + 1)
        for m in range(8):
            pa, par = pA.next()
            pb_, pbr = pB.next()
            for kc in range(4):
                mm(pa[:], wab[:, kc, m * 128:(m + 1) * 128], ot_in[sl][:, kc, :], kc == 0, kc == 3, [r_c3, r_otin[sl]], [par])
            for kc in range(4):
                mm(pb_[:], wab[:, 4 + kc, m * 128:(m + 1) * 128], ot_in[sl][:, 4 + kc, :], kc == 0, kc == 3,
                   [r_c3, r_otin[sl]], [pbr])
            a1, a1r = u1R.next()
            a2, a2r = u2R.next()
            P.add("dve", lambda e, a1=a1, pa=pa, m=m: e.tensor_tensor(out=a1[:], in0=pa[:], in1=gt_in[sl][:, m, :], op=ALU.mult),
                  reads=[par, r_gtin[sl]], writes=[a1r])
            P.add("dve", lambda e, a2=a2, pb_=pb_, m=m: e.tensor_tensor(out=a2[:], in0=pb_[:], in1=gt_in[sl][:, 8 + m, :],
                                                                       op=ALU.mult), reads=[pbr, r_gtin[sl]], writes=[a2r])
            P.add("pool", lambda e, a1=a1, a2=a2, m=m: e.tensor_tensor(out=mT[:, m, :], in0=a1[:], in1=a2[:], op=ALU.add),
                  reads=[a1r, a2r], writes=[r_mT])
        if prev_b is not None:
            prev_b()
        for tt in range(4):
            for c2 in range(2):
                po, por = pO.next()
                for kc in range(8):
                    mm(po[:], mT[:, kc, tt * 128:(tt + 1) * 128], wo[:, kc, c2 * 512:(c2 + 1) * 512], kc == 0, kc == 7,
                       [r_mT, r_c3], [por])
                P.add("dve", lambda e, po=po, tt=tt, c2=c2: e.tensor_tensor(
                    out=x_in[sl][:, tt, c2 * 512:(c2 + 1) * 512], in0=po[:], in1=x_in[sl][:, tt, c2 * 512:(c2 + 1) * 512],
                    op=ALU.add), reads=[por, r_x3[sl]], writes=[r_x3[sl]])
        P.add("sp", lambda e, ci=ci: e.dma_start(out=X1[ci * 512:(ci + 1) * 512, :].rearrange("(t p) f -> p t f", p=128),
                                                 in_=x_in[sl][:]), reads=[r_x3[sl]], writes=[], dma=True, semkey=f"x1st{sl}")
        for tt in range(4):
            P.add("act", lambda e, tt=tt: e.activation(out=junk3[:], in_=x_in[sl][:, tt, :], func=AF.Square,
                                                       accum_out=ss3[:, tt:tt + 1]), reads=[r_x3[sl]], writes=[r_j3, r_ss3])
        P.add("dve", lambda e: e.tensor_scalar(out=rs3[:], in0=ss3[:], scalar1=1.0 / D, scalar2=1e-6, op0=ALU.mult, op1=ALU.add),
              reads=[r_ss3], writes=[r_rs3])
        P.add("act", lambda e: e.activation(out=rs3[:], in_=rs3[:], func=AF.Ln), reads=[r_rs3], writes=[r_rs3])
        P.add("act", lambda e: e.activation(out=rs3[:], in_=rs3[:], func=AF.Exp, scale=-0.5), reads=[r_rs3], writes=[r_rs3])
        h2, r_h2 = h2s[sl], r_h2s[sl]
        for tt in range(4):
            P.add("act", lambda e, tt=tt: e.activation(out=h2[:, tt, :], in_=x_in[sl][:, tt, :], func=AF.Copy,
                                                       scale=rs3[:, tt:tt + 1]), reads=[r_x3[sl], r_rs3], writes=[r_h2])

        def part_b():
            st, str_ = h2R.next()
            key = f"h2st{h2R.i % 2}"
            for fc in range(8):
                pt, ptr = pT3.next()
                ptb = pt.bitcast(BF16)
                for tt in range(4):
                    P.add("pe", lambda e, tt=tt, fc=fc, ptb=ptb: e.transpose(
                        out=ptb[:, tt * 128:(tt + 1) * 128], in_=h2[:, tt, fc * 128:(fc + 1) * 128], identity=idb3[:]),
                        reads=[r_h2, r_c3], writes=[ptr])
                P.add("dve", lambda e, fc=fc, ptb=ptb: e.tensor_copy(out=st[:, fc, :], in_=ptb[:, 0:512]),
                      reads=[ptr], writes=[str_])
            P.add("sp", lambda e: e.dma_start(
                out=H2T[b, :, cc * 512:(cc + 1) * 512].rearrange("(c p) s -> p c s", p=128), in_=st[:]),
                reads=[str_], writes=[], dma=True, semkey=key)
        return part_b

    s3_load(0)
    pb_ = None
    for ci in range(NCI):
        pb_ = s3_chunk(ci, pb_)
    pb_()
    P.fence()
    if stop_after == 3:
        P.add("sp", None)
        P.emit()
        return nc

    A = Alloc(nc, "s4")
    w1 = A.t("w1", [128, 8, DFF], BF16)
    w2 = A.t("w2", [128, 32, D], BF16)
    rl = [A.t(f"rl{i}", [128, 512], F32) for i in range(2)]
    gmlp_sb = A.t("gmlp", [128, 8], F32)
    gfin_sb = A.t("gfin", [128, D], F32)
    h2t_in = [A.t(f"h2tin{i}", [128, 8, 512], BF16) for i in range(2)]
    x1_in = A.t("x1in", [128, 4, D], F32)
    uT = A.t("uT", [128, 32, 512], BF16)
    junk4 = A.t("junk", [128, D], BF16)
    ss4 = A.t("ss", [128, 4], F32)
    rs4 = A.t("rs", [128, 4], F32)
    r_c4 = P.res("const4")
    r_w1, r_w2 = P.res(), P.res()
    rlR = Rot([(rl[i], P.res()) for i in range(2)])
    r_h2in = [P.res() for _ in range(2)]
    r_x1 = P.res()
    r_uT, r_j4, r_ss4, r_rs4 = P.res(), P.res(), P.res(), P.res()
    pF = Rot([(ps[i], psr[i]) for i in range(4)])
    pG = Rot([(ps[4 + i], psr[4 + i]) for i in range(4)])
    P.add("sp", lambda e: e.dma_start(out=gmlp_sb[:], in_=gmlp), writes=[r_c4], dma=True, semkey="c4_g")
    P.add("sp", lambda e: e.dma_start(out=gfin_sb[:], in_=gfin), writes=[r_c4], dma=True, semkey="c4_gf")
    P.add("pool", lambda e: e.dma_start(out=w2[:], in_=w_ff2.rearrange("(c p) n -> p c n", p=128)),
          writes=[r_w2], dma=True, semkey="c4_w2")
    r_w1k = [P.res() for _ in range(8)]
    for kc in range(8):
        P.add("pool", lambda e, kc=kc: e.dma_start(out=w1[:, kc, :], in_=w_ff1[kc * 128:(kc + 1) * 128, :]),
              writes=[r_w1k[kc]], dma=True, semkey=f"w1_{kc}")
        if kc % 2:
            P.add("dve", lambda e, kc=kc: e.tensor_scalar_mul(out=w1[:, kc, :], in0=w1[:, kc, :], scalar1=gmlp_sb[:, kc:kc + 1]),
                  reads=[r_w1k[kc], r_c4], writes=[r_w1k[kc], r_w1])
        else:
            P.add("act", lambda e, kc=kc: e.activation(out=w1[:, kc, :], in_=w1[:, kc, :], func=AF.Copy,
                                                        scale=gmlp_sb[:, kc:kc + 1]),
                  reads=[r_w1k[kc], r_c4], writes=[r_w1k[kc], r_w1])

    def s4_load_h(ci):
        sl = ci % 2
        b, cc = divmod(ci, NCH)
        P.add("sp", lambda e: e.dma_start(out=h2t_in[sl][:], in_=H2T[b, :, cc * 512:(cc + 1) * 512].rearrange(
            "(c p) s -> p c s", p=128)), writes=[r_h2in[sl]], dma=True, semkey=f"h2in{sl}")

    def s4_chunk(ci):
        sl = ci % 2
        if ci + 1 < NCI:
            s4_load_h(ci + 1)
        P.add("sp", lambda e, ci=ci: e.dma_start(out=x1_in[:], in_=X1[ci * 512:(ci + 1) * 512, :].rearrange(
            "(t p) f -> p t f", p=128)), writes=[r_x1], dma=True, semkey="x1in")
        for m in range(32):
            pf, pfr = pF.next()
            for kc in range(8):
                mm(pf[:], w1[:, kc, m * 128:(m + 1) * 128], h2t_in[sl][:, kc, :], kc == 0, kc == 7, [r_w1, r_h2in[sl]], [pfr])
            rb, rbr = rlR.next()
            P.add("act", lambda e, pf=pf, rb=rb: e.activation(out=rb[:], in_=pf[:], func=AF.Relu), reads=[pfr], writes=[rbr])
            P.add("dve" if m % 2 else "pool", lambda e, rb=rb, m=m: e.tensor_tensor(out=uT[:, m, :], in0=rb[:], in1=rb[:], op=ALU.mult),
                  reads=[rbr], writes=[r_uT])
        for tt in range(4):
            for c2 in range(2):
                pg, pgr = pG.next()
                for kc in range(32):
                    mm(pg[:], uT[:, kc, tt * 128:(tt + 1) * 128], w2[:, kc, c2 * 512:(c2 + 1) * 512], kc == 0, kc == 31,
                       [r_uT, r_w2], [pgr])
                P.add("dve", lambda e, pg=pg, tt=tt, c2=c2: e.tensor_tensor(
                    out=x1_in[:, tt, c2 * 512:(c2 + 1) * 512], in0=pg[:], in1=x1_in[:, tt, c2 * 512:(c2 + 1) * 512], op=ALU.add),
                    reads=[pgr, r_x1], writes=[r_x1])
        for tt in range(4):
            P.add("act", lambda e, tt=tt: e.activation(out=junk4[:], in_=x1_in[:, tt, :], func=AF.Square,
                                                       accum_out=ss4[:, tt:tt + 1]), reads=[r_x1], writes=[r_j4, r_ss4])
        P.add("dve", lambda e: e.tensor_scalar(out=rs4[:], in0=ss4[:], scalar1=1.0 / D, scalar2=1e-6, op0=ALU.mult, op1=ALU.add),
              reads=[r_ss4], writes=[r_rs4])
        P.add("act", lambda e: e.activation(out=rs4[:], in_=rs4[:], func=AF.Ln), reads=[r_rs4], writes=[r_rs4])
        P.add("act", lambda e: e.activation(out=rs4[:], in_=rs4[:], func=AF.Exp, scale=-0.5), reads=[r_rs4], writes=[r_rs4])
        for tt in range(4):
            P.add("dve", lambda e, tt=tt: e.scalar_tensor_tensor(out=x1_in[:, tt, :], in0=x1_in[:, tt, :], scalar=rs4[:, tt:tt + 1],
                                                                 in1=gfin_sb[:], op0=ALU.mult, op1=ALU.mult),
                  reads=[r_x1, r_rs4, r_c4], writes=[r_x1])
        ro = P.res()
        out_res.append(ro)
        P.add("sp", lambda e, ci=ci: e.dma_start(out=out[ci * 512:(ci + 1) * 512, :].rearrange("(t p) f -> p t f", p=128),
                                                 in_=x1_in[:]), reads=[r_x1], writes=[ro], dma=True, semkey="outst")

    s4_load_h(0)
    for ci in range(NCI):
        s4_chunk(ci)
    P.add("sp", None, reads=out_res)
    P.emit()
    return nc


def _consts():
    bf = ml_dtypes.bfloat16
    ident = np.eye(128, dtype=np.float32).astype(bf)
    sw = np.zeros((128, 128), np.float32)
    for m in range(128):
        k = m + 32 if (m % 64) < 32 else m - 32
        sw[k, m] = 1.0
    pos = np.arange(S, dtype=np.float32)
    inv_freq = (10000.0 ** (-np.arange(0, 64, 2, dtype=np.float32) / 64)).astype(np.float32)
    ang = pos[:, None] * inv_freq[None, :]
    ang = np.concatenate([ang, ang], axis=-1)
    cos = np.cos(ang).astype(np.float32).T
    sin = np.sin(ang).astype(np.float32).T
    sgn = np.where(np.arange(64) < 32, -1.0, 1.0).astype(np.float32)[:, None]
    cosT = np.ascontiguousarray(np.concatenate([cos, cos], 0))
    sinT = np.ascontiguousarray(np.concatenate([sin * sgn, sin * sgn], 0))
    k = np.arange(128)[:, None]
    c_ = np.arange(256)[None, :]
    band = (k <= c_) & (c_ <= k + 128)
    seam = band & ((k < 64) == (c_ < 128))
    mask2 = np.where(np.stack([band, seam], 1), 0.0, -30000.0).astype(np.float32)
    return dict(ident=ident, rswap=sw.astype(bf), cosT=cosT, sinT=sinT, mask2=mask2.astype(bf))


_NC_CACHE = {}


def kernel(x, w_in, w_branch_a, w_branch_b, w_out, lambda_q1, lambda_k1, lambda_q2, lambda_k2,
           diff_subln_g, norm_mix_g, norm_mlp_g, w_ff1, w_ff2, norm_final_g):
    f = lambda a: np.ascontiguousarray(np.asarray(a, dtype=np.float32))
    x = f(x)
    if "nc" not in _NC_CACHE:
        _NC_CACHE["nc"] = build()
    nc = _NC_CACHE["nc"]
    c = _consts()
    lamv = np.stack([f(lambda_q1)[0], f(lambda_k1)[0], f(lambda_q2)[0], f(lambda_k2)[0]], 0)
    shared = dict(
        w_in=f(w_in)[0], w_a=f(w_branch_a)[0], w_b=f(w_branch_b)[0], w_out=f(w_out)[0], w_ff1=f(w_ff1)[0], w_ff2=f(w_ff2)[0],
        lamv=np.ascontiguousarray(np.broadcast_to(lamv[None], (128, 4, 64))),
        subg=np.ascontiguousarray(np.broadcast_to(f(diff_subln_g)[0][None], (128, 128))),
        gmix=np.ascontiguousarray(f(norm_mix_g)[0].reshape(8, 128).T),
        gmlp=np.ascontiguousarray(f(norm_mlp_g)[0].reshape(8, 128).T),
        gfin=np.ascontiguousarray(np.broadcast_to(f(norm_final_g)[None], (128, D))),
        **c,
    )
    in_maps = []
    for i in range(NCORES):
        d = dict(shared)
        d["x"] = x[i * NB:(i + 1) * NB].reshape(NB * S, D)
        in_maps.append(d)
    res = run_bass_kernel_spmd(nc, in_maps, core_ids=list(range(NCORES)))
    outs = [np.asarray(r["out"]).reshape(NB, S, D) for r in res.results]
    return np.concatenate(outs, 0).astype(np.float32)
```
